# Optimizing a Trainium2 kernel written in Bass

```python
import jax, jax.numpy as jnp
from jax import lax
import numpy as np

D_MODEL = 1024
BATCH = 16
SEQ = 256
DEPTH = 2
DEC_BATCH = 8
DEC_SEQ = 2048
PAST_LEN = 512

GRID_W = 64
N_EVEN = (DEPTH + 1) // 2
N_ODD = DEPTH // 2
D_FF = 4 * D_MODEL
NORM_EPS = 1e-6
N_MOD = 6

D_POOL = D_MODEL // 2
POOL_WINDOWS = (2, 4, 8, 16)
POOL_GROUPS = len(POOL_WINDOWS)
POOL_GW = D_POOL // POOL_GROUPS
D_RG = D_MODEL // 2
RG_BLOCKS = 8
RG_BW = D_RG // RG_BLOCKS
RG_C = 8.0
CONV_W = 4
CONV_LEFT = 2
EV_COLS = D_POOL + 2 * D_RG

D_RW = D_MODEL // 2
RW_HEAD = 64
RW_HEADS = D_RW // RW_HEAD
RW_W_LORA = 64
RW_A_LORA = 64
RW_G_LORA = 128
RW_COLS = 3 * D_RW + RW_G_LORA + RW_W_LORA + RW_A_LORA
RW_LN_EPS = 64e-5
D_ML = D_MODEL // 2
ML_HEADS = 4
ML_DK = D_ML // ML_HEADS
ML_CHUNK = 64
ML_COLS = 4 * D_ML + 4 * ML_HEADS
OD_COLS = RW_COLS + ML_COLS

kernel_name = 'hybrid_pool_rglru_rwkv7_mlstm_diffusion_step'

F32 = jnp.float32


def rms_norm(x, w):
    xf = x.astype(F32)
    y = xf * lax.rsqrt(jnp.mean(xf * xf, axis=-1, keepdims=True) + NORM_EPS)
    return (y * w.astype(F32)).astype(x.dtype)


def head_layer_norm(x, w, b, n_heads, eps):
    B, T, C = x.shape
    xf = x.astype(F32).reshape(B, T, n_heads, C // n_heads)
    mu = jnp.mean(xf, axis=-1, keepdims=True)
    var = jnp.mean(jnp.square(xf - mu), axis=-1, keepdims=True)
    y = ((xf - mu) * lax.rsqrt(var + eps)).reshape(B, T, C) * w
    if b is not None:
        y = y + b
    return y


def _dir(t, d):
    return t if d == 0 else jnp.flip(t, axis=1)


def centred_mean(x, w, axis):
    L = x.shape[axis]
    lo_off = w // 2
    hi_off = w - 1 - lo_off
    t = jnp.arange(L)
    lo = jnp.maximum(t - lo_off, 0)
    hi = jnp.minimum(t + hi_off, L - 1)
    zshape = list(x.shape)
    zshape[axis] = 1
    cs = jnp.concatenate([jnp.zeros(zshape, x.dtype), jnp.cumsum(x, axis=axis)], axis=axis)
    s = jnp.take(cs, hi + 1, axis=axis) - jnp.take(cs, lo, axis=axis)
    cshape = [1] * x.ndim
    cshape[axis] = L
    return s / (hi - lo + 1).astype(x.dtype).reshape(cshape)


def pool_mixer(u, pool_w, pool_scale, grid):
    B, T, _ = u.shape
    ug = u.astype(F32).reshape(B, T, POOL_GROUPS, POOL_GW)
    outs = []
    for g, w in enumerate(POOL_WINDOWS):
        xg = ug[:, :, g]
        if grid:
            rows = T // GRID_W
            x2 = xg.reshape(B, rows, GRID_W, POOL_GW)
            m = centred_mean(centred_mean(x2, w, 1), w, 2).reshape(B, T, POOL_GW)
        else:
            m = centred_mean(xg, w, 1)
        outs.append(m - xg)
    d = jnp.stack(outs, axis=2).astype(u.dtype)
    y = jnp.einsum('btgc,gcd->btgd', d, pool_w).reshape(B, T, D_POOL)
    return y * pool_scale


def dw_conv(x, w, b):
    T = x.shape[1]
    xp = jnp.pad(x, ((0, 0), (CONV_LEFT, CONV_W - 1 - CONV_LEFT), (0, 0)))
    y = b
    for j in range(CONV_W):
        y = y + xp[:, j:j + T] * w[j]
    return y


def linear_scan(a, b, h0):
    b = b.at[:, 0].add(a[:, 0] * h0)

    def combine(left, right):
        a_l, b_l = left
        a_r, b_r = right
        return a_l * a_r, a_r * b_l + b_r

    _, h = lax.associative_scan(combine, (a, b), axis=1)
    return h


def rglru_dir(x, wa, ba, wx, bx, lam, h0):
    B, T, _ = x.shape
    xb = x.reshape(B, T, RG_BLOCKS, RG_BW)
    r = jax.nn.sigmoid(jnp.einsum('btnc,ncd->btnd', xb, wa.astype(F32)).reshape(B, T, D_RG) + ba)
    i = jax.nn.sigmoid(jnp.einsum('btnc,ncd->btnd', xb, wx.astype(F32)).reshape(B, T, D_RG) + bx)
    log_a = -RG_C * r * jax.nn.softplus(-lam.astype(F32))
    a = jnp.exp(log_a)
    b = jnp.sqrt(-jnp.expm1(2.0 * log_a)) * (i * x)
    return linear_scan(a, b, h0)


def even_mixer(h, P, j, grid, rg_state):
    z = h @ P['ev_w_in'][j]
    u_pool, u_rg, u_gate = jnp.split(z, [D_POOL, D_POOL + D_RG], axis=-1)
    y_pool = pool_mixer(u_pool, P['pool_w'][j], P['pool_scale'][j], grid)
    xc = dw_conv(u_rg, P['rg_conv_w'][j], P['rg_conv_b'][j]).astype(F32)
    h_sum = 0.0
    finals = []
    for d in range(2):
        hd = rglru_dir(_dir(xc, d), P['rg_wa'][j, d], P['rg_ba'][j, d], P['rg_wx'][j, d],
                       P['rg_bx'][j, d], P['rg_lam'][j, d], rg_state[:, d].astype(F32))
        finals.append(hd[:, -1])
        h_sum = h_sum + _dir(hd, d)
    y_rg = h_sum * jax.nn.gelu(u_gate.astype(F32))
    y = jnp.concatenate([y_pool.astype(h.dtype), y_rg.astype(h.dtype)], axis=-1) @ P['ev_w_out'][j]
    return y, jnp.stack(finals, axis=1)


def token_shift(z, mu):
    prev = jnp.pad(z[:, :-1], ((0, 0), (1, 0), (0, 0)))
    nxt = jnp.pad(z[:, 1:], ((0, 0), (0, 1), (0, 0)))
    return z + mu[0] * (prev - z) + mu[1] * (nxt - z)


def rwkv7_scan(r, w, k, v, kk, a, S0):
    def step(S, inp):
        r_t, w_t, k_t, v_t, kk_t, a_t = inp
        sa = jnp.einsum('bhvk,bhk->bhv', S, -kk_t)
        S = (S * w_t[:, :, None, :] + sa[..., None] * (kk_t * a_t)[:, :, None, :]
             + v_t[..., None] * k_t[:, :, None, :])
        y = jnp.einsum('bhvk,bhk->bhv', S, r_t)
        return S, y

    xs = tuple(jnp.moveaxis(t, 1, 0) for t in (r, w, k, v, kk, a))
    S, ys = lax.scan(step, S0, xs)
    return jnp.moveaxis(ys, 0, 1), S


def rwkv7_mixer(z, P, j, S_init):
    B, T, _ = z.shape
    H, N = RW_HEADS, RW_HEAD
    zs = token_shift(z, P['rw_mu'][j]).astype(F32)
    cuts = [D_RW, 2 * D_RW, 3 * D_RW, 3 * D_RW + RW_G_LORA, 3 * D_RW + RW_G_LORA + RW_W_LORA]
    r, k, v, gd, wd, ad = jnp.split(zs, cuts, axis=-1)

    def heads(t):
        return t.reshape(B, T, H, N)

    kk = heads(k * P['rw_kk'][j])
    kk = kk / jnp.maximum(jnp.sqrt(jnp.sum(kk * kk, axis=-1, keepdims=True)), 1e-12)
    g = jax.nn.sigmoid(gd) @ P['rw_g2'][j]
    rh, vh = heads(r), heads(v)
    y_sum = 0.0
    bonus = 0.0
    finals = []
    for d in range(2):
        w_log = -jax.nn.softplus(-(P['rw_w0'][j, d] + jnp.tanh(wd) @ P['rw_w2'][j, d])) - 0.5
        decay = heads(jnp.exp(-jnp.exp(w_log)))
        a = jax.nn.sigmoid(P['rw_a0'][j, d] + ad @ P['rw_a2'][j, d])
        kd = heads(k * (1.0 + (a - 1.0) * P['rw_ka'][j]))
        yd, Sd = rwkv7_scan(_dir(rh, d), _dir(decay, d), _dir(kd, d), _dir(vh, d), _dir(kk, d),
                            _dir(heads(a), d), S_init[:, d].astype(F32))
        y_sum = y_sum + _dir(yd, d)
        bonus = bonus + jnp.sum(rh * kd * P['rw_rk'][j], axis=-1, keepdims=True) * vh
        finals.append(Sd)
    y = head_layer_norm(y_sum.reshape(B, T, D_RW), P['rw_ln_w'][j], P['rw_ln_b'][j], H, RW_LN_EPS)
    y = y + bonus.reshape(B, T, D_RW)
    return y * g, jnp.stack(finals, axis=1)


def mlstm_chunkwise(q, k, v, li, lf, C, n, m):
    B, T, H, DK = q.shape
    L = ML_CHUNK
    NC = T // L

    def chunks(t):
        return jnp.moveaxis(t.reshape((B, NC, L) + t.shape[2:]), 1, 0)

    causal = jnp.tril(jnp.ones((L, L), dtype=bool))[None, :, :, None]

    def step(carry, inp):
        C, n, m = carry
        qc, kc, vc, lic, lfc = inp
        b = jnp.cumsum(lfc, axis=1)
        log_d = jnp.where(causal, b[:, :, None, :] - b[:, None, :, :] + lic[:, None, :, :], -jnp.inf)
        log_inter = b + m[:, None, :]
        m_t = jnp.maximum(log_inter, jnp.max(log_d, axis=2))
        s = jnp.einsum('bthd,bshd->btsh', qc, kc) * jnp.exp(log_d - m_t[:, :, None, :])
        w_inter = jnp.exp(log_inter - m_t)
        num = (jnp.einsum('btsh,bshd->bthd', s, vc)
               + w_inter[..., None] * jnp.einsum('bthk,bhkv->bthv', qc, C))
        den = jnp.sum(s, axis=2) + w_inter * jnp.einsum('bthk,bhk->bth', qc, n)
        h = num / jnp.maximum(jnp.abs(den), jnp.exp(-m_t))[..., None]
        m_new = m_t[:, -1]
        w_s = jnp.exp(b[:, -1:] - b + lic - m_new[:, None])
        dec = jnp.exp(b[:, -1] + m - m_new)
        C = dec[..., None, None] * C + jnp.einsum('bsh,bshk,bshv->bhkv', w_s, kc, vc)
        n = dec[..., None] * n + jnp.einsum('bsh,bshk->bhk', w_s, kc)
        return (C, n, m_new), h

    (C, n, m), hs = lax.scan(step, (C, n, m), tuple(chunks(t) for t in (q, k, v, li, lf)))
    h = jnp.moveaxis(hs, 0, 1).reshape(B, T, H, DK)
    return h, (C, n, m)


def mlstm_mixer(z, P, j, C0, n0, m0):
    B, T, _ = z.shape
    qk, v, o, gates = jnp.split(z, [2 * D_ML, 3 * D_ML, 4 * D_ML], axis=-1)
    qk = jax.nn.silu(dw_conv(qk, P['ml_conv_w'][j], P['ml_conv_b'][j])).astype(F32)
    q, k = jnp.split(qk, 2, axis=-1)

    def heads(t):
        return t.reshape(B, T, ML_HEADS, ML_DK)

    q = heads(q) * (ML_DK ** -0.5)
    k = heads(k)
    vh = heads(v.astype(F32))
    gates = gates.astype(F32).reshape(B, T, 2, 2, ML_HEADS)
    h_sum = 0.0
    Cs, ns, ms = [], [], []
    for d in range(2):
        li = gates[:, :, d, 0] + P['ml_bi'][j, d]
        lf = jax.nn.log_sigmoid(gates[:, :, d, 1] + P['ml_bf'][j, d])
        hd, (Cd, nd, md) = mlstm_chunkwise(_dir(q, d), _dir(k, d), _dir(vh, d), _dir(li, d), _dir(lf, d),
                                           C0[:, d].astype(F32), n0[:, d].astype(F32), m0[:, d].astype(F32))
        h_sum = h_sum + _dir(hd, d)
        Cs.append(Cd)
        ns.append(nd)
        ms.append(md)
    h = head_layer_norm(h_sum.reshape(B, T, D_ML), P['ml_norm_w'][j], None, ML_HEADS, NORM_EPS)
    y = h * jax.nn.sigmoid(o.astype(F32))
    return y, (jnp.stack(Cs, axis=1), jnp.stack(ns, axis=1), jnp.stack(ms, axis=1))


def odd_mixer(h, P, j, S0, C0, n0, m0):
    z = h @ P['od_w_in'][j]
    zr, zm = jnp.split(z, [RW_COLS], axis=-1)
    y_r, S = rwkv7_mixer(zr, P, j, S0)
    y_m, (C, n, m) = mlstm_mixer(zm, P, j, C0, n0, m0)
    y = jnp.concatenate([y_r, y_m], axis=-1).astype(h.dtype) @ P['od_w_out'][j]
    return y, (S, C, n, m)


def trunk(x, cond, grid, rg0, rw0, C0, n0, m0, P):
    rg_f, rw_f, C_f, n_f, m_f = [], [], [], [], []
    for l in range(DEPTH):
        j = l // 2
        mod = (jax.nn.silu(cond) @ P['w_mod'][l] + P['b_mod'][l])[:, None, :]
        sh1, sc1, g1, sh2, sc2, g2 = jnp.split(mod, N_MOD, axis=-1)
        h = rms_norm(x, P['norm1_w'][l]) * (1.0 + sc1) + sh1
        if l % 2 == 0:
            y, rg = even_mixer(h, P, j, grid, rg0[:, j])
            rg_f.append(rg)
        else:
            y, (S, C, n, m) = odd_mixer(h, P, j, rw0[:, j], C0[:, j], n0[:, j], m0[:, j])
            rw_f.append(S)
            C_f.append(C)
            n_f.append(n)
            m_f.append(m)
        x = x + g1 * y
        h = rms_norm(x, P['norm2_w'][l]) * (1.0 + sc2) + sh2
        x = x + g2 * (jnp.square(jax.nn.relu(h @ P['mlp_w1'][l])) @ P['mlp_w2'][l])
    y = rms_norm(x, P['final_norm_w'])
    return y, (jnp.stack(rg_f, axis=1), jnp.stack(rw_f, axis=1), jnp.stack(C_f, axis=1),
               jnp.stack(n_f, axis=1), jnp.stack(m_f, axis=1))


def setup_inputs(seed: int = 0) -> dict:
    key = jax.random.key(seed)
    ks = list(jax.random.split(key, 64))

    def normal(shape, scale):
        return jax.random.normal(ks.pop(), shape, F32) * scale

    def unif(shape, lo, hi):
        return jax.random.uniform(ks.pop(), shape, F32, lo, hi)

    D = D_MODEL
    lam_a = unif((N_EVEN, 2, D_RG), 0.9, 0.999)
    return {
        'x_prompt': normal((BATCH, SEQ, D), 1.0),
        'x_sample': normal((DEC_BATCH, DEC_SEQ, D), 1.0),
        'state_rglru': normal((DEC_BATCH, N_EVEN, 2, D_RG), 0.5),
        'state_rwkv': normal((DEC_BATCH, N_ODD, 2, RW_HEADS, RW_HEAD, RW_HEAD), 0.2),
        'state_mlstm_C': normal((DEC_BATCH, N_ODD, 2, ML_HEADS, ML_DK, ML_DK), 0.5),
        'state_mlstm_n': normal((DEC_BATCH, N_ODD, 2, ML_HEADS, ML_DK), 0.5),
        'state_mlstm_m': unif((DEC_BATCH, N_ODD, 2, ML_HEADS), 0.0, 3.0),
        'c': normal((DEC_BATCH, D), 1.0),
        'c_ctx': normal((D,), 1.0),
        'norm1_w': 1.0 + normal((DEPTH, D), 0.05),
        'norm2_w': 1.0 + normal((DEPTH, D), 0.05),
        'w_mod': normal((DEPTH, D, N_MOD * D), 0.5 * D ** -0.5),
        'b_mod': normal((DEPTH, N_MOD * D), 0.02),
        'mlp_w1': normal((DEPTH, D, D_FF), D ** -0.5),
        'mlp_w2': normal((DEPTH, D_FF, D), D_FF ** -0.5),
        'final_norm_w': 1.0 + normal((D,), 0.05),
        'ev_w_in': normal((N_EVEN, D, EV_COLS), D ** -0.5),
        'pool_w': normal((N_EVEN, POOL_GROUPS, POOL_GW, POOL_GW), POOL_GW ** -0.5),
        'pool_scale': 1.0 + normal((N_EVEN, D_POOL), 0.1),
        'rg_conv_w': normal((N_EVEN, CONV_W, D_RG), CONV_W ** -0.5),
        'rg_conv_b': normal((N_EVEN, D_RG), 0.02),
        'rg_wa': normal((N_EVEN, 2, RG_BLOCKS, RG_BW, RG_BW), RG_BW ** -0.5),
        'rg_ba': normal((N_EVEN, 2, D_RG), 0.1),
        'rg_wx': normal((N_EVEN, 2, RG_BLOCKS, RG_BW, RG_BW), RG_BW ** -0.5),
        'rg_bx': normal((N_EVEN, 2, D_RG), 0.1),
        'rg_lam': jnp.log(lam_a) - jnp.log1p(-lam_a),
        'ev_w_out': normal((N_EVEN, D_POOL + D_RG, D), (D_POOL + D_RG) ** -0.5),
        'od_w_in': normal((N_ODD, D, OD_COLS), D ** -0.5),
        'rw_mu': unif((N_ODD, 2, RW_COLS), 0.0, 0.5),
        'rw_w0': jnp.linspace(-6.5, -1.0, D_RW, dtype=F32) + normal((N_ODD, 2, D_RW), 0.1),
        'rw_w2': normal((N_ODD, 2, RW_W_LORA, D_RW), 0.5 * RW_W_LORA ** -0.5),
        'rw_a0': normal((N_ODD, 2, D_RW), 0.1),
        'rw_a2': normal((N_ODD, 2, RW_A_LORA, D_RW), 0.5 * RW_A_LORA ** -0.5),
        'rw_kk': 0.85 + normal((N_ODD, D_RW), 0.05),
        'rw_ka': 1.0 + normal((N_ODD, D_RW), 0.05),
        'rw_rk': normal((N_ODD, RW_HEADS, RW_HEAD), 0.1),
        'rw_g2': normal((N_ODD, RW_G_LORA, D_RW), RW_G_LORA ** -0.5),
        'rw_ln_w': 1.0 + normal((N_ODD, D_RW), 0.05),
        'rw_ln_b': normal((N_ODD, D_RW), 0.02),
        'ml_conv_w': normal((N_ODD, CONV_W, 2 * D_ML), CONV_W ** -0.5),
        'ml_conv_b': normal((N_ODD, 2 * D_ML), 0.02),
        'ml_bi': normal((N_ODD, 2, ML_HEADS), 0.1),
        'ml_bf': jnp.linspace(3.0, 6.0, ML_HEADS, dtype=F32) + normal((N_ODD, 2, ML_HEADS), 0.1),
        'ml_norm_w': 1.0 + normal((N_ODD, D_ML), 0.05),
        'od_w_out': normal((N_ODD, D_RW + D_ML, D), (D_RW + D_ML) ** -0.5),
    }


def reference(x_prompt, x_sample, state_rglru, state_rwkv, state_mlstm_C, state_mlstm_n, state_mlstm_m,
              c, c_ctx, norm1_w, norm2_w, w_mod, b_mod, mlp_w1, mlp_w2, final_norm_w,
              ev_w_in, pool_w, pool_scale, rg_conv_w, rg_conv_b, rg_wa, rg_ba, rg_wx, rg_bx, rg_lam, ev_w_out,
              od_w_in, rw_mu, rw_w0, rw_w2, rw_a0, rw_a2, rw_kk, rw_ka, rw_rk, rw_g2, rw_ln_w, rw_ln_b,
              ml_conv_w, ml_conv_b, ml_bi, ml_bf, ml_norm_w, od_w_out):
    P = dict(norm1_w=norm1_w, norm2_w=norm2_w, w_mod=w_mod, b_mod=b_mod, mlp_w1=mlp_w1, mlp_w2=mlp_w2,
             final_norm_w=final_norm_w, ev_w_in=ev_w_in, pool_w=pool_w, pool_scale=pool_scale,
             rg_conv_w=rg_conv_w, rg_conv_b=rg_conv_b, rg_wa=rg_wa, rg_ba=rg_ba, rg_wx=rg_wx, rg_bx=rg_bx,
             rg_lam=rg_lam, ev_w_out=ev_w_out, od_w_in=od_w_in, rw_mu=rw_mu, rw_w0=rw_w0, rw_w2=rw_w2,
             rw_a0=rw_a0, rw_a2=rw_a2, rw_kk=rw_kk, rw_ka=rw_ka, rw_rk=rw_rk, rw_g2=rw_g2,
             rw_ln_w=rw_ln_w, rw_ln_b=rw_ln_b, ml_conv_w=ml_conv_w, ml_conv_b=ml_conv_b, ml_bi=ml_bi,
             ml_bf=ml_bf, ml_norm_w=ml_norm_w, od_w_out=od_w_out)
    Bp = x_prompt.shape[0]
    rg0 = jnp.zeros((Bp, N_EVEN, 2, D_RG), F32)
    rw0 = jnp.zeros((Bp, N_ODD, 2, RW_HEADS, RW_HEAD, RW_HEAD), F32)
    C0 = jnp.zeros((Bp, N_ODD, 2, ML_HEADS, ML_DK, ML_DK), F32)
    n0 = jnp.zeros((Bp, N_ODD, 2, ML_HEADS, ML_DK), F32)
    m0 = jnp.zeros((Bp, N_ODD, 2, ML_HEADS), F32)
    cond_ctx = jnp.broadcast_to(c_ctx[None, :], (Bp, D_MODEL))
    y_prompt, (s_rg, s_rw, s_C, s_n, s_m) = trunk(x_prompt, cond_ctx, False, rg0, rw0, C0, n0, m0, P)
    y_sample, _ = trunk(x_sample, c, True, state_rglru, state_rwkv, state_mlstm_C, state_mlstm_n,
                        state_mlstm_m, P)
    return (y_prompt, y_sample, s_rg, s_rw, s_C, s_n, s_m)
```

```python
import os
import numpy as np
import concourse.bass as bass
import concourse.mybir as mybir
from concourse.bass_utils import run_bass_kernel_spmd
from contextlib import ExitStack

F32 = mybir.dt.float32
BF16 = mybir.dt.bfloat16
AF = mybir.ActivationFunctionType
SOFTPLUS = AF.Exp if os.environ.get("K_NOSP") else AF.Softplus
ALU = mybir.AluOpType
AX = mybir.AxisListType

NCORES = 8
D = 1024
NTOK = 2560
SEQ_LEN = [256, 256, 2048]
XOFF = [0, 256, 512]
HPAD = 2
HOFF = [2, 262, 522]
HTOT = 2572
TILES = [(0, 0, 256), (1, 0, 256), (2, 0, 512), (2, 512, 512), (2, 1024, 512), (2, 1536, 512)]
NT = len(TILES)
EPS = 1e-6


NAMES = {}


class Sem:
    def __init__(self, h):
        self.h = h
        self.cnt = 0


class Eng:
    def __init__(self, name, e, sem, is_pe=False):
        self.name, self.e, self.sem, self.is_pe = name, e, sem, is_pe
        self.cnt = 0
        self.seen = {}


class Buf:
    def __init__(self, t, name=""):
        self.t = t
        self.name = name
        self.w = None
        self.r = {}
        self.dsem = None
        self.dcnt = 0
        self.is_dram = False
        self.allw = {}

    def __getitem__(self, idx):
        return V(self.t[idx], self)


class V:
    def __init__(self, ap, buf):
        self.ap, self.buf = ap, buf

    def __getitem__(self, idx):
        return V(self.ap[idx], self.buf)

    def re(self, s, **kw):
        return V(self.ap.rearrange(s, **kw), self.buf)


class K:
    def __init__(self, nc, es):
        self.nc, self.es = nc, es
        self.nsem = 0
        self.free_sems = {}
        self.scopes = []
        self.allbufs = []
        self.free_sems = {}
        mk = self.newsem
        self.pe = Eng("pe", nc.tensor, mk("pe"), True)
        self.act = Eng("act", nc.scalar, mk("act"))
        self.dve = Eng("dve", nc.vector, mk("dve"))
        self.pool = Eng("pool", nc.gpsimd, mk("pool"))
        self.sp = Eng("sp", nc.sync, mk("sp"))
        self.rr = 0

    def newsem(self, name, kind="hw"):
        fl = self.free_sems.setdefault(kind, []) if isinstance(getattr(self, "free_sems", None), dict) else None
        if fl:
            return fl.pop()
        self.nsem += 1
        sm = Sem(self.es.enter_context(self.nc.semaphore(f"s_{name}_{self.nsem}")))
        sm.kind = kind
        return sm

    def sb(self, name, shape, dt=F32):
        self.uid = getattr(self, "uid", 0) + 1
        es = self.scopes[-1] if getattr(self, "scopes", None) else self.es
        t = es.enter_context(self.nc.sbuf_tensor(f"sb_{name}_{self.uid}", list(shape), dt))
        b = Buf(t, name)
        self.allbufs.append(b)
        NAMES[name] = f"sb_{name}_{self.uid}"
        return b

    def barrier(self):
        engs = [self.pe, self.act, self.dve, self.pool, self.sp]
        for E in engs:
            for X in engs:
                if X is E or X.cnt == 0:
                    continue
                if E.seen.get(X.sem, 0) < X.cnt:
                    E.e.wait_ge(X.sem.h, X.cnt)
                    E.seen[X.sem] = X.cnt
            for b in self.allbufs:
                if b.dsem is not None and E.seen.get(b.dsem, 0) < b.dsem.cnt:
                    E.e.wait_ge(b.dsem.h, b.dsem.cnt)
                    E.seen[b.dsem] = b.dsem.cnt

    def scope(self):
        k = self

        class _S:
            def __enter__(s_):
                st = ExitStack()
                st.__enter__()
                k.scopes.append(st)
                s_.nb = len(k.allbufs)

            def __exit__(s_, *a):
                if a[0] is not None:
                    return False
                k.barrier()
                for b in k.allbufs[s_.nb:]:
                    if b.dsem is not None:
                        k.free_sems.setdefault(b.dsem.kind, []).append(b.dsem)
                        b.dsem = None
                del k.allbufs[s_.nb:]
                st = k.scopes.pop()
                st.__exit__(None, None, None)
                return False
        return _S()

    def view(self, parent, name=""):
        return Buf(parent.t, name)

    def _waits(self, E, reads, writes, dma_sem=None):
        waits = {}

        def need(ev):
            if ev is None:
                return
            s, v = ev
            if waits.get(s, 0) < v:
                waits[s] = v

        for b in reads:
            need(b.w)
        for b in writes:
            if not (dma_sem is not None and b.w is not None and b.w[0] is dma_sem):
                need(b.w)
            for s, v in b.r.items():
                need((s, v))
        for s, v in waits.items():
            if s is E.sem and E.is_pe:
                continue
            if E.seen.get(s, 0) >= v:
                continue
            E.e.wait_ge(s.h, v)
            E.seen[s] = v

    def emit(self, E, reads, writes, fn):
        reads = [b for b in reads if b is not None]
        self._waits(E, reads, writes)
        inst = fn()
        E.cnt += 1
        inst.then_inc(E.sem.h, 1)
        ev = (E.sem, E.cnt)
        for b in writes:
            b.w = ev
            b.r = {}
        for b in reads:
            if b not in writes:
                b.r[E.sem] = E.cnt
        return inst

    def dma(self, Q, out, in_, semof=None, **kw):
        sbuf = semof if semof is not None else (in_.buf if out.buf.is_dram else out.buf)
        qkind = "sw" if Q is self.pool else "hw"
        if sbuf.dsem is None:
            sbuf.dsem = self.newsem("d" + sbuf.name, qkind)
        assert sbuf.dsem.kind == qkind, (sbuf.name, qkind)
        rd = [] if in_.buf.is_dram else [in_.buf]
        wr = [] if out.buf.is_dram else [out.buf]
        self._waits(Q, rd, wr, dma_sem=sbuf.dsem)
        inst = Q.e.dma_start(out=out.ap, in_=in_.ap, **kw)
        sbuf.dsem.cnt += 16
        inst.then_inc(sbuf.dsem.h, 16)
        ev = (sbuf.dsem, sbuf.dsem.cnt)
        if out.buf.is_dram:
            out.buf.allw[sbuf.dsem] = sbuf.dsem.cnt
        else:
            out.buf.w = ev
            out.buf.r = {}
        if not in_.buf.is_dram:
            in_.buf.r[sbuf.dsem] = sbuf.dsem.cnt

    def _bufs(self, *vs):
        return [v.buf for v in vs if isinstance(v, V)]

    @staticmethod
    def _a(v):
        return v.ap if isinstance(v, V) else v

    def mm(self, out, lhsT, rhs, start=True, stop=True):
        return self.emit(self.pe, [lhsT.buf, rhs.buf], [out.buf],
                         lambda: self.nc.tensor.matmul(out.ap, lhsT.ap, rhs.ap, start=start, stop=stop))

    def tr(self, out, in_, ident):
        return self.emit(self.pe, [in_.buf, ident.buf], [out.buf],
                         lambda: self.nc.tensor.transpose(out.ap, in_.ap, ident.ap))

    def actf(self, out, in_, func, bias=None, scale=None, accum=None, E=None):
        kw = {}
        if bias is not None:
            kw["bias"] = self._a(bias)
        if scale is not None:
            kw["scale"] = self._a(scale)
        if accum is not None:
            kw["accum_out"] = accum.ap
        wr = [out.buf] + ([accum.buf] if accum is not None else [])
        return self.emit(self.act, self._bufs(in_, bias, scale), wr,
                         lambda: self.nc.scalar.activation(out=out.ap, in_=in_.ap, func=func, **kw))

    def softplus(self, out, in_, bias=None, scale=None, composite=False):
        if os.environ.get("K_NOSP") or composite:
            self.actf(out, in_, AF.Exp, bias=bias, scale=scale)
            return self.actf(out, out, AF.Ln, bias=self.onecol[0:out.ap.shape[0], :])
        return self.actf(out, in_, AF.Softplus, bias=bias, scale=scale)

    def _ve(self, E):
        return E if E is not None else self.dve

    def tt(self, out, in0, in1, op, E=None):
        E = self._ve(E)
        return self.emit(E, [in0.buf, in1.buf], [out.buf],
                         lambda: E.e.tensor_tensor(out=out.ap, in0=in0.ap, in1=in1.ap, op=op))

    def ts(self, out, in0, s1, op0, s2=None, op1=None, E=None):
        E = self._ve(E)
        kw = {}
        if op1 is not None:
            kw["op1"] = op1
        return self.emit(E, self._bufs(in0, s1, s2), [out.buf],
                         lambda: E.e.tensor_scalar(out=out.ap, in0=in0.ap, scalar1=self._a(s1), scalar2=self._a(s2),
                                                   op0=op0, **kw))

    def stt(self, out, in0, scalar, in1, op0, op1):
        return self.emit(self.dve, self._bufs(in0, scalar, in1), [out.buf],
                         lambda: self.nc.vector.scalar_tensor_tensor(out=out.ap, in0=in0.ap, scalar=self._a(scalar),
                                                                     in1=in1.ap, op0=op0, op1=op1))

    def scan(self, out, d0, d1, init, op0, op1):
        return self.emit(self.dve, self._bufs(d0, d1, init), [out.buf],
                         lambda: self.nc.vector.tensor_tensor_scan(out=out.ap, data0=d0.ap, data1=d1.ap,
                                                                   initial=self._a(init), op0=op0, op1=op1))

    def copy(self, out, in_, E=None):
        E = self._ve(E)
        if E is self.act:
            return self.emit(E, [in_.buf], [out.buf], lambda: self.nc.scalar.copy(out=out.ap, in_=in_.ap))
        return self.emit(E, [in_.buf], [out.buf], lambda: E.e.tensor_copy(out=out.ap, in_=in_.ap))

    def memset(self, out, val, E=None):
        E = self._ve(E)
        return self.emit(E, [], [out.buf], lambda: E.e.memset(out.ap, val))

    def recip(self, out, in_):
        return self.emit(self.dve, [in_.buf], [out.buf], lambda: self.nc.vector.reciprocal(out=out.ap, in_=in_.ap))

    def anyE(self):
        self.rr += 1
        return self.dve if self.rr % 2 else self.pool

    def finish(self, bufs):
        waits = {}
        for b in bufs:
            for s, v in b.allw.items():
                waits[s] = max(waits.get(s, 0), s.cnt)
        for s, v in waits.items():
            self.nc.sync.wait_ge(s.h, v)


class Pack:
    def __init__(self):
        self.cols = []
        self.idx = {}
        self.n = 0

    def add(self, name, vec):
        vec = np.asarray(vec, np.float32).reshape(-1)
        k = vec.size // 128
        assert k * 128 == vec.size, name
        self.idx[name] = (self.n, k)
        self.cols.append(vec.reshape(k, 128).T)
        self.n += k

    def add_rows(self, name, arr):
        arr = np.asarray(arr, np.float32)
        self.idx[name] = (self.n, arr.shape[1])
        self.cols.append(arr)
        self.n += arr.shape[1]

    def build(self):
        return np.ascontiguousarray(np.concatenate(self.cols, axis=1))


def make_consts():
    c = {}
    c["ident"] = np.eye(128, dtype=np.float32)
    i = np.arange(128)
    c["blk64"] = ((i[:, None] // 64) == (i[None, :] // 64)).astype(np.float32)
    c["ones"] = np.ones((128, 128), np.float32)
    c["up_incl"] = (i[:, None] <= i[None, :]).astype(np.float32)
    c["up_strict"] = (i[:, None] < i[None, :]).astype(np.float32)
    c["lo_incl"] = (i[:, None] >= i[None, :]).astype(np.float32)
    c["lo_strict"] = (i[:, None] > i[None, :]).astype(np.float32)
    for j in range(8):
        sel = np.zeros((128, 128), np.float32)
        sel[(j // 4) * 32 + (j % 4), :] = 1.0
        c[f"sel{j}"] = sel
    for lv in range(7):
        b = 1 << lv
        m = ((i[:, None] // (2 * b)) == (i[None, :] // (2 * b))) & ((i[:, None] // b) != (i[None, :] // b)) & (i[:, None] > i[None, :])
        c[f"lvL{lv}"] = m.astype(np.float32)
    for lv in range(7):
        c[f"lvU{lv}"] = np.ascontiguousarray(c[f"lvL{lv}"].T)
    names = list(c.keys())
    arr = np.concatenate([c[n] for n in names], axis=1)
    return names, np.ascontiguousarray(arr)


CONST_NAMES, CONST_ARR = make_consts()


def pack_params(inp, core):
    P = Pack()
    b = core
    P.add("c_ctx", inp["c_ctx"])
    P.add("c_s", inp["c"][b])
    for l in range(2):
        P.add(f"n1w{l}", inp["norm1_w"][l])
        P.add(f"n2w{l}", inp["norm2_w"][l])
        P.add(f"bmod{l}", inp["b_mod"][l])
    P.add("fnw", inp["final_norm_w"])
    P.add("pool_scale", inp["pool_scale"][0])
    for j in range(4):
        P.add(f"rg_cw{j}", inp["rg_conv_w"][0, j])
    P.add("rg_cb", inp["rg_conv_b"][0])
    for d in range(2):
        P.add(f"rg_ba{d}", inp["rg_ba"][0, d])
        P.add(f"rg_bx{d}", inp["rg_bx"][0, d])
        P.add(f"rg_lam{d}", inp["rg_lam"][0, d])
        P.add(f"rg_st{d}", inp["state_rglru"][b, 0, d])
    for j in range(4):
        P.add(f"ml_cw{j}", inp["ml_conv_w"][0, j])
    P.add("ml_cb", inp["ml_conv_b"][0])
    P.add("ml_nw", inp["ml_norm_w"][0])
    bi = np.zeros((128, 1), np.float32); bf = np.zeros((128, 1), np.float32); m0 = np.zeros((128, 1), np.float32)
    for d in range(2):
        for h in range(4):
            bi[d * 32 + h, 0] = inp["ml_bi"][0, d, h]
            bf[d * 32 + h, 0] = inp["ml_bf"][0, d, h]
            m0[d * 32 + h, 0] = inp["state_mlstm_m"][b, 0, d, h]
    P.add_rows("ml_bi", bi); P.add_rows("ml_bf", bf); P.add_rows("ml_m0", m0)
    P.add("rw_mu0", inp["rw_mu"][0, 0]); P.add("rw_mu1", inp["rw_mu"][0, 1])
    for d in range(2):
        P.add(f"rw_w0{d}", inp["rw_w0"][0, d]); P.add(f"rw_a0{d}", inp["rw_a0"][0, d])
    P.add("rw_kk", inp["rw_kk"][0]); P.add("rw_ka", inp["rw_ka"][0]); P.add("rw_rk", inp["rw_rk"][0].reshape(-1))
    P.add("rw_lnw", inp["rw_ln_w"][0]); P.add("rw_lnb", inp["rw_ln_b"][0])
    return P


def build(pidx, npar, stage):
    nc = bass.Bass("TRN2", target_bir_lowering=False)
    dram = {}

    def din(name, shape, dt=F32):
        dram[name] = Buf(nc.dram_tensor(name, list(shape), dt, kind="ExternalInput").ap(), name)
        dram[name].is_dram = True
        return dram[name]

    def dout(name, shape):
        dram[name] = Buf(nc.dram_tensor(name, list(shape), F32, kind="ExternalOutput").ap(), name)
        dram[name].is_dram = True
        return dram[name]

    d_xp = din("xp", [512, D])
    d_xs = din("xs", [2048, D])
    d_par = din("par", [128, npar])
    d_cst = din("cst", list(CONST_ARR.shape))
    d_wmod = din("w_mod", [2 * D, 6 * D])
    d_w1 = din("mlp_w1", [2 * D, 4 * D])
    d_w2 = din("mlp_w2", [2 * 4 * D, D])
    d_evin = din("ev_w_in", [D, 1536])
    d_evout = din("ev_w_out", [D, D])
    d_poolw = din("pool_w", [512, 128])
    d_rgwa = din("rg_wa", [1024, 64])
    d_rgwx = din("rg_wx", [1024, 64])
    d_odin = din("od_w_in", [D, 3856])
    d_odout = din("od_w_out", [D, D])
    d_rww2 = din("rw_w2", [128, 512])
    d_rwa2 = din("rw_a2", [128, 512])
    d_rwg2 = din("rw_g2", [128, 512])
    d_strw = din("st_rw", [2, 8, 64, 64])
    d_stC = din("st_C", [2, 4, 128, 128])
    d_stn = din("st_n", [2, 4, 128])
    o_yp = dout("yp", [512, D])
    o_ys = dout("ys", [2048, D])
    o_srg = dout("s_rg", [2, 2, 512])
    o_srw = dout("s_rw", [2, 2, 8, 64, 64])
    o_sC = dout("s_C", [2, 2, 4, 128, 128])
    o_sn = dout("s_n", [2, 2, 4, 128])
    o_sm = dout("s_m", [2, 2, 4])
    outs = [o_yp, o_ys, o_srg, o_srw, o_sC, o_sn, o_sm]

    with ExitStack() as es:
        k = K(nc, es)
        pe, act, dve, pool, sp = k.pe, k.act, k.dve, k.pool, k.sp
        k.allbufs.extend(dram.values())

        par = k.sb("par", [128, npar])
        cst = k.sb("cst", [128, 7 * 128])
        cstb = k.sb("cstb", [128, 384], BF16)
        xT = k.sb("xT", [128, 8, NTOK])
        hT = k.sb("hT", [128, 8, HTOT], BF16)
        XB = [[k.view(xT, f"x{c}_{t}") for t in range(NT)] for c in range(8)]
        HB = [k.view(hT, f"h{t}") for t in range(NT)]
        PS = [Buf(es.enter_context(nc.psum_tensor(f"ps{i}", [128, 512], F32)), f"ps{i}") for i in range(8)]
        psi = [0]

        ps_reserve = [False]

        def ps():
            psi[0] = (psi[0] + 1) % 8
            if ps_reserve[0] and psi[0] == 0:
                psi[0] = 1
            return PS[psi[0]]

        def P(name, j=0, n=1):
            o, kk = pidx[name]
            return par[:, o + j:o + j + n]

        def C(name, bf=False):
            i = CONST_NAMES.index(name)
            return (cstb if bf else cst)[:, i * 128:(i + 1) * 128]

        def Xv(c, ti):
            s, st, n = TILES[ti]
            return XB[c][ti][:, c, XOFF[s] + st:XOFF[s] + st + n]

        def Hv(c, ti):
            s, st, n = TILES[ti]
            return HB[ti][:, c, HOFF[s] + st:HOFF[s] + st + n]

        def Hany(c, col0, n):
            return V(hT.t[:, c, col0:col0 + n], HB[0])

        def h_reads():
            return HB

        onecol_b = k.sb("onecol", [128, 1])
        k.memset(onecol_b[:, :], 1.0)
        k.onecol = onecol_b[:, :]
        k.dma(sp, par[:, :], d_par[:, :])
        k.dma(sp, cst[:, :], d_cst[:, 0:7 * 128])
        k.copy(cstb[:, :], cst[:, 0:384])
        k.memset(hT[:, :, :].re("p a b -> p (a b)"), 0.0, E=pool)
        for t in range(1, NT):
            HB[t].w = HB[0].w

        WSL = [k.sb(f"wsl{i}", [128, 8, 512], BF16) for i in range(2)]
        wsi = [0]

        def wslot():
            wsi[0] ^= 1
            return WSL[wsi[0]]

        def load_w(slot, dW, r0, kc, c0, n, dst_c0=0):
            src = dW.t[r0:r0 + kc * 128, c0:c0 + n].rearrange("(k p) n -> p k n", p=128)
            k.dma(pool, slot[:, 0:kc, dst_c0:dst_c0 + n], V(src, dW))

        scT = k.sb("scT", [128, 8, 2], BF16)
        k.actf(scT[:, :, 0], P("c_ctx", 0, 8), AF.Silu)
        k.actf(scT[:, :, 1], P("c_s", 0, 8), AF.Silu)
        modt = k.sb("modt", [128, 2, 48, 2])
        for l in range(2):
            pm = ps()
            for g in range(12):
                sl = wslot()
                load_w(sl, d_wmod, l * D, 8, g * 512, 512)
                for f in range(4):
                    j = g * 4 + f
                    for kc in range(8):
                        k.mm(pm[:, j * 2:j * 2 + 2], sl[:, kc, f * 128:(f + 1) * 128], scT[:, kc, :],
                             start=(kc == 0), stop=(kc == 7))
            o, _ = pidx[f"bmod{l}"]
            k.tt(modt[:, l, :, :], pm[:, 0:96].re("p (j c) -> p j c", c=2),
                 V(par.t[:, o:o + 48].unsqueeze(2).broadcast_to([128, 48, 2]), par), ALU.add)
        modA = k.sb("modA", [128, 2, 2, 8, 2])
        for l in range(2):
            for w_, (nw, scj) in enumerate([(f"n1w{l}", 8), (f"n2w{l}", 32)]):
                o, _ = pidx[nw]
                nwb = V(par.t[:, o:o + 8].unsqueeze(2).broadcast_to([128, 8, 2]), par)
                k.stt(modA[:, l, w_, :, :], modt[:, l, scj:scj + 8, :], 1.0, nwb, ALU.add, ALU.mult)

        def modB(l, w_, c, cond):
            j = (0 if w_ == 0 else 24) + c
            return modt[:, l, j, cond:cond + 1]

        def modG(l, w_, c, cond):
            j = (16 if w_ == 0 else 40) + c
            return modt[:, l, j, cond:cond + 1]

        k.scopes.append(ExitStack())
        xin = [k.sb(f"xin{i}", [128, D]) for i in range(2)]
        for tb in range(20):
            src = d_xp if tb < 4 else d_xs
            r0 = tb * 128 if tb < 4 else (tb - 4) * 128
            xi = xin[tb % 2]
            k.dma(sp, xi[:, :], src[r0:r0 + 128, :])
            col0 = tb * 128
            ti = [i for i, (s, st, n) in enumerate(TILES) if XOFF[s] + st <= col0 < XOFF[s] + st + n][0]
            for half in range(2):
                pt = ps()
                for q in range(4):
                    c = half * 4 + q
                    k.tr(pt[:, q * 128:(q + 1) * 128], xi[:, c * 128:(c + 1) * 128], C("ident"))
                bufs = [XB[half * 4 + q][ti] for q in range(4)]
                dst = V(xT.t[:, half * 4:half * 4 + 4, col0:col0 + 128], bufs[0])
                E = act if half == 0 else dve
                k._waits(E, [], bufs[1:])
                k.copy(dst, pt[:, :].re("p (q n) -> p q n", q=4), E=E)
                for bb in bufs[1:]:
                    bb.w = bufs[0].w
                    bb.r = {}

        k.barrier()
        k.scopes.pop().close()
        nb = {}
        cnt = {"sq": 0, "rs": 0, "nt": 0}

        def norm_alloc():
            nb["sqb"] = [k.sb(f"sqb{i}", [128, 512], BF16) for i in range(8)]
            nb["rstd"] = [k.sb(f"rstd{i}", [128, 512]) for i in range(2)]
            nb["ntmp"] = [k.sb(f"ntmp{i}", [128, 512]) for i in range(8)]

        def rot(lst, key):
            cnt[key] += 1
            return lst[cnt[key] % len(lst)]

        def norm_tile(ti):
            s, st, n = TILES[ti]
            pss = ps()
            for c in range(8):
                sq = rot(nb["sqb"], "sq")
                k.actf(sq[:, :n], Xv(c, ti), AF.Square)
                k.mm(pss[:, :n], C("ones", True), sq[:, :n], start=(c == 0), stop=(c == 7))
            rs = rot(nb["rstd"], "rs")
            k.actf(rs[:, :n], pss[:, :n], AF.Sqrt, bias=P("eps"), scale=1.0 / D)
            k.recip(rs[:, :n], rs[:, :n])
            return rs

        def norm_mod(l, w_):
          with k.scope():
            norm_alloc()
            for ti in range(NT):
                s, st, n = TILES[ti]
                cond = 0 if s < 2 else 1
                rs = norm_tile(ti)
                for c in range(8):
                    tmp = rot(nb["ntmp"], "nt")
                    k.tt(tmp[:, :n], Xv(c, ti), rs[:, :n], ALU.mult, E=k.anyE())
                    k.actf(Hv(c, ti), tmp[:, :n], AF.Identity, bias=modB(l, w_, c, cond),
                           scale=modA[:, l, w_, c, cond:cond + 1])

        def x_accum(c, ti, psv, l, w_):
            s, st, n = TILES[ti]
            cond = 0 if s < 2 else 1
            k.stt(Xv(c, ti), psv, modG(l, w_, c, cond), Xv(c, ti), ALU.mult, ALU.add)

        mc = {"h": 0, "r": 0}

        def mlp(l):
          with k.scope():
            hid = [k.sb(f"hid{i}", [128, 4, 512], BF16) for i in range(2)]
            w2s = [k.sb(f"w2s{i}", [128, 4, 1024], BF16) for i in range(2)]
            rl = [k.sb(f"rl{i}", [128, 512]) for i in range(2)]
            S1, S2, HBUF = {}, {}, {}

            def load(g):
                s1 = wslot()
                load_w(s1, d_w1, l * D, 8, g * 512, 512)
                s2 = w2s[g % 2]
                src = d_w2.t[l * 4 * D + g * 512:l * 4 * D + (g + 1) * 512, :].rearrange("(k p) n -> p k n", p=128)
                k.dma(pool, s2[:, :, :], V(src, d_w2))
                S1[g], S2[g] = s1, s2

            units = [(g, ti) for g in range(8) for ti in range(NT)]

            def mm1(u):
                g, ti = units[u]
                s, st, n = TILES[ti]
                hb = hid[u % 2]
                HBUF[u] = hb
                for j in range(4):
                    p1 = ps()
                    for kc in range(8):
                        k.mm(p1[:, :n], S1[g][:, kc, j * 128:(j + 1) * 128], Hv(kc, ti), start=(kc == 0), stop=(kc == 7))
                    mc["r"] += 1
                    r_ = rl[mc["r"] % 2]
                    k.actf(r_[:, :n], p1[:, :n], AF.Relu)
                    k.tt(hb[:, j, :n], r_[:, :n], r_[:, :n], ALU.mult)

            def mm2(u):
                g, ti = units[u]
                s, st, n = TILES[ti]
                hb = HBUF[u]
                for oc in range(8):
                    p2 = ps()
                    for j in range(4):
                        k.mm(p2[:, :n], S2[g][:, j, oc * 128:(oc + 1) * 128], hb[:, j, :n], start=(j == 0), stop=(j == 3))
                    x_accum(oc, ti, p2[:, :n], l, 1)
                if ti == NT - 1 and g + 2 < 8:
                    load(g + 2)

            load(0)
            load(1)
            mm1(0)
            for u in range(len(units)):
                if u + 1 < len(units):
                    mm1(u + 1)
                mm2(u)

        def out_proj(dW, chunk, yv_of_tile, wout, l):
            wo = wout[chunk % 2]
            k.dma(pool, wo[:, :], dW[chunk * 128:(chunk + 1) * 128, :])
            for ti in range(NT):
                s, st, n = TILES[ti]
                for oc in range(8):
                    p2 = ps()
                    k.mm(p2[:, :n], wo[:, oc * 128:(oc + 1) * 128], yv_of_tile(ti))
                    x_accum(oc, ti, p2[:, :n], l, 0)

        def box1d(src, nouter, L, unit, w, A, B):
            lo = w // 2
            Lp = L + w - 1
            tot = nouter * Lp * unit
            av = A[:, 0:tot].re("p (o l u) -> p o l u", o=nouter, l=Lp, u=unit)
            bv = B[:, 0:tot].re("p (o l u) -> p o l u", o=nouter, l=Lp, u=unit)
            k.memset(A[:, 0:tot], 0.0, E=pool)
            k.copy(av[:, :, lo:lo + L, :], src, E=pool)
            cur, oth = av, bv
            span = Lp
            sh = 1
            while sh < w:
                span -= sh
                k.tt(oth[:, :, 0:span, :], cur[:, :, 0:span, :], cur[:, :, sh:sh + span, :], ALU.add, E=k.anyE())
                cur, oth = oth, cur
                sh *= 2
            assert span == L
            return cur[:, :, 0:L, :], oth

        def v4(v, o, l, u):
            return v.re("p (o l u) -> p o l u", o=o, l=l, u=u)

        def layer0_pool():
          l = 0
          with k.scope():
            ymix = k.sb("ymix", [128, NTOK], BF16)
            wout = [k.sb(f"wout{i}", [128, D], BF16) for i in range(2)]

            def ymix_tile(ti):
                s, st, n = TILES[ti]
                return ymix[:, XOFF[s] + st:XOFF[s] + st + n]
            ubuf = k.sb("ubuf", [128, NTOK])
            M1 = k.sb("M1", [128, 2048])
            pa = k.sb("pa", [128, 1536])
            pb = k.sb("pb", [128, 1536])
            pw = k.sb("pw", [128, 128], BF16)
            dbf = k.sb("dbf", [128, NTOK], BF16)
            rc = k.sb("rc", [128, 3, 256])
            onesb = k.sb("onesb", [128, 256])
            k.memset(onesb[:, :], 1.0)
            for g in range(4):
                w = 2 << g
                for kind, L in enumerate([256, 32, 64]):
                    r, _ = box1d(v4(onesb[:, 0:L], 1, L, 1), 1, L, 1, w, pa, pb)
                    k.recip(v4(rc[:, kind, 0:L], 1, L, 1), r)
                sl = wslot()
                load_w(sl, d_evin, 0, 8, g * 128, 128)
                k.dma(pool, pw[:, :], d_poolw[g * 128:(g + 1) * 128, :])
                for ti in range(NT):
                    s, st, n = TILES[ti]
                    p1 = ps()
                    for kc in range(8):
                        k.mm(p1[:, :n], sl[:, kc, 0:128], Hv(kc, ti), start=(kc == 0), stop=(kc == 7))
                    k.copy(ubuf[:, XOFF[s] + st:XOFF[s] + st + n], p1[:, :n], E=act)
                for s in range(2):
                    uv = ubuf[:, XOFF[s]:XOFF[s] + 256]
                    r, oth = box1d(v4(uv, 1, 256, 1), 1, 256, 1, w, pa, pb)
                    m = oth[:, :, 0:256, :]
                    k.tt(m, r, v4(rc[:, 0, 0:256], 1, 256, 1), ALU.mult)
                    k.tt(v4(dbf[:, XOFF[s]:XOFF[s] + 256], 1, 256, 1), m, v4(uv, 1, 256, 1), ALU.subtract, E=pool)
                u3 = ubuf[:, 512:2560].re("p (l u) -> p l u", u=64)
                m13 = M1[:, :].re("p (l u) -> p l u", u=64)
                for hf in range(2):
                    src = u3[:, :, hf * 32:(hf + 1) * 32]
                    r, _ = box1d(V(src.ap.unsqueeze(1), src.buf), 1, 32, 32, w, pa, pb)
                    rcr = V(rc.t[:, 1, 0:32].unsqueeze(2).broadcast_to([128, 32, 32]), rc)
                    k.tt(m13[:, :, hf * 32:(hf + 1) * 32], r[:, 0, :, :], rcr, ALU.mult)
                d3 = dbf[:, 512:2560].re("p (l u) -> p l u", u=64)
                for hf in range(2):
                    src = m13[:, hf * 16:(hf + 1) * 16, :]
                    r2, oth = box1d(V(src.ap.unsqueeze(3), src.buf), 16, 64, 1, w, pa, pb)
                    rcc = V(rc.t[:, 2, 0:64].unsqueeze(1).broadcast_to([128, 16, 64]), rc)
                    m2 = oth[:, :, 0:64, 0]
                    k.tt(m2, r2[:, :, :, 0], rcc, ALU.mult)
                    k.tt(d3[:, hf * 16:(hf + 1) * 16, :], m2, u3[:, hf * 16:(hf + 1) * 16, :], ALU.subtract, E=pool)
                for ti in range(NT):
                    s, st, n = TILES[ti]
                    p1 = ps()
                    k.mm(p1[:, :n], pw[:, :], dbf[:, XOFF[s] + st:XOFF[s] + st + n])
                    k.actf(ymix_tile(ti), p1[:, :n], AF.Copy, scale=P("pool_scale", g))
                out_proj(d_evout, g, ymix_tile, wout, l)

        def layer0_rg():
          l = 0
          with k.scope():
            ymix = k.sb("ymix", [128, NTOK], BF16)
            wout = [k.sb(f"wout{i}", [128, D], BF16) for i in range(2)]

            def ymix_tile(ti):
                s, st, n = TILES[ti]
                return ymix[:, XOFF[s] + st:XOFF[s] + st + n]
            UP = k.sb("UP", [128, HTOT])
            HF = k.sb("HF", [128, NTOK])
            GEL = k.sb("GEL", [128, NTOK])
            wblk = k.sb("wblk", [128, 4, 128], BF16)
            wstg = k.sb("wstg", [128, 4, 64])
            nsp = k.sb("nsp", [128, 4])
            srg = k.sb("srg", [128, 2, 2, 4])
            seg = {n_: k.sb(f"sg_{n_}", [128, 512]) for n_ in ["xc", "r", "i", "s", "hb"]}
            xb = k.sb("xcb", [128, 512], BF16)
            RGT = [(seg["xc"], seg["r"], seg["i"], seg["s"], xb),
                   (k.sb("sg_xc1", [128, 512]), k.sb("sg_r1", [128, 512]), k.sb("sg_i1", [128, 512]), k.sb("sg_s1", [128, 512]),
                    k.sb("xcb1", [128, 512], BF16))]
            k.memset(UP[:, :], 0.0, E=pool)
            k.memset(wblk[:, :, :].re("p a b -> p (a b)"), 0.0, E=pool)
            carry = k.sb("carry", [128, 4])
            for c in range(4):
                sl = wslot()
                load_w(sl, d_evin, 0, 8, 512 + c * 128, 128, 0)
                load_w(sl, d_evin, 0, 8, 1024 + c * 128, 128, 128)
                for d in range(2):
                    for gi, dWg in enumerate([d_rgwa, d_rgwx]):
                        r0 = (d * 8 + 2 * c) * 64
                        k.dma(sp, wstg[:, d * 2 + gi, :], dWg[r0:r0 + 128, :])
                for d in range(2):
                    for gi in range(2):
                        k.copy(wblk[0:64, d * 2 + gi, 0:64], wstg[0:64, d * 2 + gi, :])
                        k.copy(wblk[64:128, d * 2 + gi, 64:128], wstg[64:128, d * 2 + gi, :])
                    k.softplus(nsp[:, d * 2:d * 2 + 1], P(f"rg_lam{d}", c), scale=-1.0)
                    k.ts(nsp[:, d * 2:d * 2 + 1], nsp[:, d * 2:d * 2 + 1], -8.0, ALU.mult)
                    k.ts(nsp[:, d * 2 + 1:d * 2 + 2], nsp[:, d * 2:d * 2 + 1], 2.0, ALU.mult)
                for ti in range(NT):
                    s, st, n = TILES[ti]
                    p1 = ps()
                    p2 = ps()
                    for kc in range(8):
                        k.mm(p1[:, :n], sl[:, kc, 0:128], Hv(kc, ti), start=(kc == 0), stop=(kc == 7))
                    for kc in range(8):
                        k.mm(p2[:, :n], sl[:, kc, 128:256], Hv(kc, ti), start=(kc == 0), stop=(kc == 7))
                    k.copy(UP[:, HOFF[s] + st:HOFF[s] + st + n], p1[:, :n], E=act)
                    k.actf(GEL[:, XOFF[s] + st:XOFF[s] + st + n], p2[:, :n], AF.Gelu_apprx_tanh)
                def rgA(d, ti, T):
                    xc, r_, i_, s_, xb_ = T
                    s, st, n = TILES[ti]
                    h0 = HOFF[s] + st
                    k.ts(xc[:, :n], UP[:, h0 - 2:h0 - 2 + n], P("rg_cw0", c), ALU.mult, P("rg_cb", c), ALU.add)
                    for j in range(1, 4):
                        k.stt(xc[:, :n], UP[:, h0 - 2 + j:h0 - 2 + j + n], P(f"rg_cw{j}", c), xc[:, :n], ALU.mult, ALU.add)
                    k.copy(xb_[:, :n], xc[:, :n], E=act)
                    pr = ps()
                    pi = ps()
                    k.mm(pr[:, :n], wblk[:, d * 2, :], xb_[:, :n])
                    k.mm(pi[:, :n], wblk[:, d * 2 + 1, :], xb_[:, :n])
                    k.actf(r_[:, :n], pr[:, :n], AF.Sigmoid, bias=P(f"rg_ba{d}", c))
                    k.actf(i_[:, :n], pi[:, :n], AF.Sigmoid, bias=P(f"rg_bx{d}", c))
                    k.actf(s_[:, :n], r_[:, :n], AF.Exp, scale=nsp[:, d * 2 + 1:d * 2 + 2])
                    k.actf(r_[:, :n], r_[:, :n], AF.Exp, scale=nsp[:, d * 2:d * 2 + 1])
                    k.tt(i_[:, :n], i_[:, :n], xc[:, :n], ALU.mult)
                    k.ts(s_[:, :n], s_[:, :n], -1.0, ALU.mult, 1.0, ALU.add)
                    k.ts(s_[:, :n], s_[:, :n], 0.0, ALU.max)
                    k.actf(s_[:, :n], s_[:, :n], AF.Sqrt)
                    k.tt(i_[:, :n], i_[:, :n], s_[:, :n], ALU.mult)

                def rgB(d, ti, T):
                    xc, r_, i_, s_, xb_ = T
                    hb = seg["hb"]
                    s, st, n = TILES[ti]
                    first = (st == 0) if d == 0 else (st + n == SEQ_LEN[s])
                    if first:
                        init = P(f"rg_st{d}", c) if s == 2 else 0.0
                    else:
                        init = carry[:, d:d + 1]
                    x0 = XOFF[s] + st
                    if d == 0:
                        dst = HF[:, x0:x0 + n]
                        k.scan(dst, r_[:, :n], i_[:, :n], init, ALU.mult, ALU.add)
                        k.copy(carry[:, 0:1], HF[:, x0 + n - 1:x0 + n])
                        if s < 2 and st + n == SEQ_LEN[s]:
                            k.copy(srg[:, s, 0, c:c + 1], HF[:, x0 + n - 1:x0 + n])
                    else:
                        k.scan(V(hb.t[:, 0:n][:, ::-1], hb), V(r_.t[:, 0:n][:, ::-1], r_),
                               V(i_.t[:, 0:n][:, ::-1], i_), init, ALU.mult, ALU.add)
                        k.copy(carry[:, 1:2], hb[:, 0:1])
                        if s < 2 and st == 0:
                            k.copy(srg[:, s, 1, c:c + 1], hb[:, 0:1])
                        k.tt(hb[:, :n], hb[:, :n], HF[:, x0:x0 + n], ALU.add)
                        k.tt(ymix[:, x0:x0 + n], hb[:, :n], GEL[:, x0:x0 + n], ALU.mult)

                for d in range(2):
                    order = list(range(NT)) if d == 0 else list(range(NT - 1, -1, -1))
                    rgA(d, order[0], RGT[0])
                    for j_, ti in enumerate(order):
                        if j_ + 1 < len(order):
                            rgA(d, order[j_ + 1], RGT[(j_ + 1) % 2])
                        rgB(d, ti, RGT[j_ % 2])
                out_proj(d_evout, 4 + c, ymix_tile, wout, l)
            for s in range(2):
                for d in range(2):
                    dst = o_srg.t[s, d, :].rearrange("(c p) -> p c", p=128)
                    k.dma(sp, V(dst, o_srg), srg[:, s, d, :], allow_slow_non_contiguous=True)

        def layer0_mixers():
            layer0_pool()
            layer0_rg()

        CHUNKS = [(s_, st_) for s_ in range(3) for st_ in range(0, SEQ_LEN[s_], 128)]

        def psb16(buf):
            return V(buf.t[:, :].bitcast(BF16), buf)

        def bn(out_mv, in_, st6):
            k.emit(dve, [in_.buf], [st6.buf], lambda: nc.vector.bn_stats(out=st6.ap, in_=in_.ap))
            k.emit(dve, [st6.buf], [out_mv.buf], lambda: nc.vector.bn_aggr(out=out_mv.ap, in_=st6.ap))

        def rev(v):
            return V(v.ap[:, ::-1], v.buf)

        def layer1_mlstm():
          l = 1
          with k.scope():
            ymix = k.sb("ymix", [128, NTOK], BF16)
            wout = [k.sb(f"wout{i}", [128, D], BF16) for i in range(2)]

            def ymix_tile(ti):
                s, st, n = TILES[ti]
                return ymix[:, XOFF[s] + st:XOFF[s] + st + n]
            selt = k.sb("selt", [128, 8 * 128])
            k.dma(sp, selt[:, :], d_cst[:, 7 * 128:15 * 128])
            GR = k.sb("GR", [128, HTOT])
            TK = k.sb("TK", [128, 20, 3, 8])
            SMT = k.sb("SMT", [128, 2])
            onec = k.sb("onec", [128, 1])
            nbf = k.sb("nbf", [128, 1])
            k.memset(onec[:, :], 1.0)
            k.ts(nbf[:, :], P("ml_bf"), -1.0, ALU.mult)
            with k.scope():
                LI = k.sb("LI", [128, HTOT])
                LF = k.sb("LF", [128, HTOT])
                FF = k.sb("FF", [128, HTOT])
                gst = k.sb("gst", [128, 8, 16])
                sg = k.sb("sg", [128, 8, 2, 128], BF16)
                k.dma(sp, gst[:, :, :], V(d_odin.t[:, 3840:3856].rearrange("(k p) n -> p k n", p=128), d_odin))
                k.memset(sg[:, :, :, :].re("p a b c -> p (a b c)"), 0.0, E=pool)
                for d in range(2):
                    for gi in range(2):
                        k.copy(sg[:, :, gi, d * 32:d * 32 + 4], gst[:, :, d * 8 + gi * 4:d * 8 + gi * 4 + 4])
                k.memset(LI[:, :], 0.0, E=pool)
                k.memset(LF[:, :], 0.0, E=pool)
                k.memset(FF[:, :], 0.0, E=pool)
                k.memset(GR[:, :], 0.0, E=pool)
                for ti in range(NT):
                    s, st, n = TILES[ti]
                    pli = ps()
                    plf = ps()
                    for kc in range(8):
                        k.mm(pli[:, :n], sg[:, kc, 0, :], Hv(kc, ti), start=(kc == 0), stop=(kc == 7))
                    for kc in range(8):
                        k.mm(plf[:, :n], sg[:, kc, 1, :], Hv(kc, ti), start=(kc == 0), stop=(kc == 7))
                    hc = HOFF[s] + st
                    k.actf(LI[0:64, hc:hc + n], pli[0:64, :n], AF.Identity, bias=P("ml_bi")[0:64, :])
                    k.softplus(LF[0:64, hc:hc + n], plf[0:64, :n], bias=nbf[0:64, :], scale=-1.0)
                    k.ts(LF[0:64, hc:hc + n], LF[0:64, hc:hc + n], -1.0, ALU.mult, E=pool)
                for s in range(3):
                    n = SEQ_LEN[s]
                    hc = HOFF[s]
                    for d in range(2):
                        R_ = slice(d * 32, d * 32 + 4)
                        o = (lambda v: v) if d == 0 else rev
                        ones_b = V(onec.t[R_, 0:1].broadcast_to([4, n]), onec)
                        k.scan(o(FF[R_, hc:hc + n]), ones_b, o(LF[R_, hc:hc + n]), 0.0, ALU.mult, ALU.add)
                        k.tt(LI[R_, hc:hc + n], LI[R_, hc:hc + n], FF[R_, hc:hc + n], ALU.subtract)
                        init = P("ml_m0")[R_, :] if s == 2 else 0.0
                        k.scan(o(GR[R_, hc:hc + n]), o(LI[R_, hc:hc + n]), o(LI[R_, hc:hc + n]), init, ALU.max, ALU.max)
                        k.tt(FF[R_, hc:hc + n], FF[R_, hc:hc + n], GR[R_, hc:hc + n], ALU.add)
                        halo = hc - 1 if d == 0 else hc + n
                        if s == 2:
                            k.copy(GR[R_, halo:halo + 1], P("ml_m0")[R_, :])
                        if s < 2:
                            fin = hc + n - 1 if d == 0 else hc
                            k.copy(SMT[R_, s:s + 1], FF[R_, fin:fin + 1])
                for ci, (s, st) in enumerate(CHUNKS):
                    hc = HOFF[s] + st
                    pt = ps()
                    for q, RB in enumerate([LI, GR, FF]):
                        k.tr(pt[:, q * 128:(q + 1) * 128], RB[:, hc:hc + 128], C("ident"))
                    src = V(pt.t[:, 0:384].rearrange("p (q a b) -> p q a b", q=3, a=4)[:, :, 0:2, 0:4], pt)
                    k.copy(TK[:, ci, :, :].re("p q (a b) -> p q a b", a=2), src, E=act)
                for s in range(2):
                    for d in range(2):
                        k.dma(sp, V(o_sm.t[s, d, :].rearrange("(h o) -> h o", o=1), o_sm), SMT[d * 32:d * 32 + 4, s:s + 1])
            QT = k.sb("QT", [128, NTOK], BF16)
            KT_ = k.sb("KT", [128, NTOK], BF16)
            KTOK = k.sb("KTOK", [128, 20, 128], BF16)
            V1 = k.sb("V1", [128, 20, 130], BF16)
            HS = k.sb("HS", [128, 20, 128])
            cacc = k.sb("cacc", [128, 128])
            qtmp = k.sb("qtmp", [128, 128])
            TS = [(k.sb(f"Eb{i_}", [128, 128]), k.sb(f"Pb{i_}", [128, 128], BF16), k.sb(f"kw{i_}", [128, 128], BF16),
                   k.sb(f"cols{i_}", [128, 8])) for i_ in range(2)]
            T1 = k.sb("T1", [128, 130])
            ND = k.sb("ND", [128, 130])
            CN = [k.sb(f"CN{s}", [128, 130]) for s in range(3)]
            CNb = [k.sb(f"CNb{s}", [128, 130], BF16) for s in range(3)]
            st6 = k.sb("st6", [128, 6])
            mv = k.sb("mv", [128, 2])
            hn = k.sb("hn", [128, 128])
            sgo = k.sb("sgo", [128, 128])
            k.memset(V1[:, :, :].re("p a b -> p (a b)"), 1.0, E=pool)
            for h in range(4):
                sl = wslot()
                for q in range(4):
                    load_w(sl, d_odin, 0, 8, 1792 + q * 512 + h * 128, 128, q * 128)
                caq = [cacc, TS[0][0]]
                cak = [hn, TS[1][0]]
                qtm = [qtmp, sgo]

                def m1X(ci, p_):
                    s, st = CHUNKS[ci]
                    hc = HOFF[s] + st
                    x0 = XOFF[s] + st
                    pz = ps()
                    for q in range(2):
                        for kc in range(8):
                            k.mm(pz[:, q * 131:q * 131 + 131], sl[:, kc, q * 128:(q + 1) * 128], Hany(kc, hc - 2, 131),
                                 start=(kc == 0), stop=(kc == 7))
                    pv = ps()
                    for kc in range(8):
                        k.mm(pv[:, 0:128], Hany(kc, hc, 128), sl[:, kc, 256:384], start=(kc == 0), stop=(kc == 7))
                    for q in range(2):
                        cj = q * 4 + h
                        z = pz[:, q * 131:q * 131 + 131]
                        acc = (caq if q == 0 else cak)[p_]
                        k.ts(acc[:, :], z[:, 0:128], P("ml_cw0", cj), ALU.mult, P("ml_cb", cj), ALU.add)
                        for j in range(1, 4):
                            k.stt(acc[:, :], z[:, j:j + 128], P(f"ml_cw{j}", cj), acc[:, :], ALU.mult, ALU.add)
                        if q == 0:
                            k.actf(qtm[p_][:, :], acc[:, :], AF.Silu)
                            k.ts(QT[:, x0:x0 + 128], qtm[p_][:, :], float(128 ** -0.5), ALU.mult, E=pool)
                        else:
                            k.actf(KT_[:, x0:x0 + 128], acc[:, :], AF.Silu)
                    k.copy(V1[:, ci, 0:128], pv[:, 0:128], E=act)
                    pk = ps()
                    pk16 = psb16(pk)
                    k.tr(pk16[:, 0:128], KT_[:, x0:x0 + 128], C("ident", True))
                    return pk16

                pk_cur = m1X(0, 0)
                for ci in range(len(CHUNKS)):
                    pk_nxt = m1X(ci + 1, (ci + 1) % 2) if ci + 1 < len(CHUNKS) else None
                    k.copy(KTOK[:, ci, :], pk_cur[:, 0:128])
                    pk_cur = pk_nxt
                def stepA(d, s, ci, T):
                    Eb, Pb, kw, cols = T
                    dh = d * 4 + h
                    _, st = CHUNKS[ci]
                    hc = HOFF[s] + st
                    x0 = XOFF[s] + st
                    c0, cL = (0, 128) if d == 0 else (129, 1)
                    pa_ = ps()
                    k.mm(pa_[:, 0:128], KT_[:, x0:x0 + 128], QT[:, x0:x0 + 128])
                    pb_ = ps()
                    k.mm(pb_[:, 0:130], selt[:, dh * 128:(dh + 1) * 128], GR[:, hc - 1:hc + 129])
                    k.actf(Eb[:, :], pb_[:, 1:129], AF.Exp, bias=TK[:, ci, 0, dh:dh + 1], scale=-1.0)
                    k.copy(cols[:, 0:1], pb_[:, c0:c0 + 1])
                    k.ts(cols[:, 1:2], pb_[:, cL:cL + 1], -1.0, ALU.mult)
                    k.tt(Eb[:, :], Eb[:, :], C("up_incl" if d == 0 else "lo_incl"), ALU.mult)
                    k.actf(cols[:, 2:3], TK[:, ci, 1, dh:dh + 1], AF.Exp, bias=cols[:, 0:1], scale=-1.0)
                    k.actf(cols[:, 3:4], TK[:, ci, 2, dh:dh + 1], AF.Exp, scale=-1.0)
                    k.actf(cols[:, 6:7], TK[:, ci, 0, dh:dh + 1], AF.Exp, bias=cols[:, 1:2])
                    k.actf(cols[:, 7:8], cols[:, 0:1], AF.Exp, bias=cols[:, 1:2])
                    k.tt(Pb[:, :], pa_[:, 0:128], Eb[:, :], ALU.mult)
                    k.ts(kw[:, :], KTOK[:, ci, :], cols[:, 6:7], ALU.mult)

                def stepB(d, s, ci, T, cn, cnb):
                    Eb, Pb, kw, cols = T
                    _, st = CHUNKS[ci]
                    x0 = XOFF[s] + st
                    pd_ = ps()
                    k.mm(pd_[:, 0:129], QT[:, x0:x0 + 128], cnb[:, 0:129])
                    pe_ = ps()
                    k.mm(pe_[:, 0:129], kw[:, :], V1[:, ci, 0:129])
                    pc_ = ps()
                    k.mm(pc_[:, 0:129], Pb[:, :], V1[:, ci, 0:129])
                    k.actf(T1[:, 0:129], pd_[:, 0:129], AF.Identity, scale=cols[:, 2:3])
                    k.stt(cn[:, 0:129], cn[:, 0:129], cols[:, 7:8], pe_[:, 0:129], ALU.mult, ALU.add)
                    k.copy(cnb[:, 0:129], cn[:, 0:129], E=act)
                    k.tt(ND[:, 0:129], pc_[:, 0:129], T1[:, 0:129], ALU.add)
                    k.ts(cols[:, 4:5], ND[:, 128:129], -1.0, ALU.mult)
                    k.tt(cols[:, 4:5], cols[:, 4:5], ND[:, 128:129], ALU.max)
                    k.ts(cols[:, 4:5], cols[:, 4:5], cols[:, 3:4], ALU.max)
                    k.recip(cols[:, 5:6], cols[:, 4:5])
                    if d == 0:
                        k.ts(HS[:, ci, :], ND[:, 0:128], cols[:, 5:6], ALU.mult)
                    else:
                        k.stt(HS[:, ci, :], ND[:, 0:128], cols[:, 5:6], HS[:, ci, :], ALU.mult, ALU.add)

                for d in range(2):
                    for s in range(3):
                        cn, cnb = CN[s], CNb[s]
                        if s == 2:
                            k.dma(sp, cn[:, 0:128], d_stC[d, h, :, :])
                            k.dma(sp, cn[:, 128:129], V(d_stn.t[d, h, :].rearrange("(p o) -> p o", o=1), d_stn))
                        else:
                            k.memset(cn[:, :], 0.0)
                        k.copy(cnb[:, 0:129], cn[:, 0:129], E=act)
                        cis = [ci for ci, (s_, st_) in enumerate(CHUNKS) if s_ == s]
                        if d == 1:
                            cis = cis[::-1]
                        stepA(d, s, cis[0], TS[0])
                        for j_, ci in enumerate(cis):
                            if j_ + 1 < len(cis):
                                stepA(d, s, cis[j_ + 1], TS[(j_ + 1) % 2])
                            stepB(d, s, ci, TS[j_ % 2], cn, cnb)
                        if s < 2:
                            k.dma(sp, o_sC[s, d, h, :, :], cn[:, 0:128])
                            k.dma(sp, V(o_sn.t[s, d, h, :].rearrange("(p o) -> p o", o=1), o_sn), cn[:, 128:129])
                hnP = [hn, TS[0][0]]
                sgP = [sgo, TS[1][0]]
                mvP = [mv, T1]
                s6P = [st6, ND]

                def m3X(ci, par):
                    mvp = mvP[par]
                    bn(mvp[:, 0:2], HS[:, ci, :], s6P[par][:, 0:6])
                    k.actf(mvp[:, 1:2], mvp[:, 1:2], AF.Sqrt, bias=P("eps"))
                    k.recip(mvp[:, 1:2], mvp[:, 1:2])
                    k.ts(hnP[par][:, :], HS[:, ci, :], mvp[:, 0:1], ALU.subtract, mvp[:, 1:2], ALU.mult)

                def m3Y(ci, par):
                    s, st = CHUNKS[ci]
                    hc = HOFF[s] + st
                    x0 = XOFF[s] + st
                    po = ps()
                    for kc in range(8):
                        k.mm(po[:, 0:128], sl[:, kc, 384:512], Hany(kc, hc, 128), start=(kc == 0), stop=(kc == 7))
                    k.actf(sgP[par][:, :], po[:, 0:128], AF.Sigmoid)
                    pf_ = ps()
                    k.tr(pf_[:, 0:128], hnP[par][:, :], C("ident"))
                    k.stt(ymix[:, x0:x0 + 128], pf_[:, 0:128], P("ml_nw", h), sgP[par][:, :], ALU.mult, ALU.mult)

                m3X(0, 0)
                for ci in range(len(CHUNKS)):
                    if ci + 1 < len(CHUNKS):
                        m3X(ci + 1, (ci + 1) % 2)
                    m3Y(ci, ci % 2)
                out_proj(d_odout, 4 + h, ymix_tile, wout, l)

        def layer1_rwkv():
          l = 1
          with k.scope():
            ymix = k.sb("ymix", [128, NTOK], BF16)
            wout1 = k.sb("wout0", [128, D], BF16)
            wout = [wout1, wout1]

            def ymix_tile(ti):
                s, st, n = TILES[ti]
                return ymix[:, XOFF[s] + st:XOFF[s] + st + n]
            LW = k.sb("LW", [128, NTOK], BF16)
            LG = k.sb("LG", [128, NTOK], BF16)
            YA = k.sb("YA", [128, 20, 128])
            BON = k.sb("BON", [128, NTOK], BF16)
            lvm = k.sb("lvm", [128, 14 * 128], BF16)
            k.dma(pool, lvm[:, :], d_cst[:, 15 * 128:29 * 128])
            if os.environ.get("K_RWSMALL") or os.environ.get("K_RWDBG"):
                k.memset(YA[:, :, :].re("p a b -> p (a b)"), 0.0)
                k.memset(BON[:, :], 0.0)
            L2w = k.sb("L2w", [128, 2, 512], BF16)
            L2a = k.sb("L2a", [128, 2, 512], BF16)
            k.memset(L2w[:, :, :].re("p a b -> p (a b)"), 0.0)
            k.memset(L2a[:, :, :].re("p a b -> p (a b)"), 0.0)
            G2 = k.sb("G2", [128, 512], BF16)
            muc = k.sb("muc", [128, 14])
            nw0 = k.sb("nw0", [128, 2, 4])
            na0 = k.sb("na0", [128, 2, 4])
            cc = k.sb("cc", [128, 4])
            k.memset(cc[:, 0:1], -0.5)
            k.memset(cc[:, 1:2], 1.0)
            k.memset(cc[:, 2:3], 64e-5)
            for d in range(2):
                k.dma(pool, L2w[0:64, d, :], d_rww2[d * 64:(d + 1) * 64, :])
                k.dma(pool, L2a[64:128, d, :], d_rwa2[d * 64:(d + 1) * 64, :])
                k.ts(nw0[:, d, :], P(f"rw_w0{d}", 0, 4), -1.0, ALU.mult)
                k.ts(na0[:, d, :], P(f"rw_a0{d}", 0, 4), -1.0, ALU.mult)
            k.dma(pool, G2[:, :], d_rwg2[:, :])
            k.tt(muc[:, :], P("rw_mu0", 0, 14), P("rw_mu1", 0, 14), ALU.add)
            k.ts(muc[:, :], muc[:, :], -1.0, ALU.mult, 1.0, ALU.add)
            F = {n_: k.sb(f"rw_{n_}", [128, 128]) for n_ in
                 ["r", "k", "v", "sq", "kk", "e", "a", "cum", "E1", "E2", "E3", "E4", "t1", "kd", "ba"]}
            F["cumx"] = F["sq"]
            F["nrm"] = F["t1"]
            F["kkr"] = F["E4"]
            B = {n_: k.sb(f"rwb_{n_}", [128, 128], BF16) for n_ in ["AT", "BT", "KT", "RT", "BW", "KW", "vb"]}
            TKM2 = [k.sb(f"TKM{i_}", [128, 4, 128], BF16) for i_ in range(2)]
            WL = k.sb("rwWL", [128, 2])
            HB_ = []
            for hh in range(2):
                d_ = {n_: k.sb(f"rwh{hh}_{n_}", [128, 128], BF16) for n_ in
                      ["M0", "M1", "MT0", "MT1", "XT0", "XT1", "AAK"]}
                for n_ in ["ARB", "ARK", "XAK", "XAT", "RTz"]:
                    d_[n_] = [k.sb(f"rwh{hh}_{n_}{i_}", [128, 128], BF16) for i_ in range(2)]
                d_["XTf"] = k.sb(f"rwh{hh}_XTf", [128, 128])
                d_["Xf"] = k.sb(f"rwh{hh}_Xf", [128, 128])
                d_["ATz"] = k.sb(f"rwh{hh}_ATz", [128, 128], BF16)
                k.memset(d_["ATz"][:, :], 0.0)
                for i_ in range(2):
                    k.memset(d_["RTz"][i_][:, :], 0.0)
                    k.memset(d_["XAT"][i_][:, :], 0.0)
                sbz1 = k.sb(f"rwh{hh}_Sbz", [128, 64], BF16)
                k.memset(sbz1[:, :], 0.0)
                d_["Sbz"] = [sbz1, sbz1, sbz1]
                d_["Ub"] = k.sb(f"rwh{hh}_Ub", [128, 64], BF16)
                HB_.append(d_)
            st1 = k.sb("rwS", [128, 128])
            k.memset(st1[:, :], 0.0)
            ST = [st1, st1, st1]
            STb = [None] * 3
            stg = F["ba"]
            st6 = k.sb("rwst6", [128, 6])
            mv = k.sb("rwmv", [128, 4])
            yn = k.sb("rwyn", [128, 128])
            sto = yn

            def shift(dst, z, j):
                k.ts(dst, z[:, 1:129], muc[:, j:j + 1], ALU.mult)
                k.stt(dst, z[:, 0:128], P("rw_mu0", j), dst, ALU.mult, ALU.add)
                k.stt(dst, z[:, 2:130], P("rw_mu1", j), dst, ALU.mult, ALU.add)

            sl = wslot()
            load_w(sl, d_odin, 0, 8, 1536, 256, 0)
            for ci, (s, st) in enumerate(CHUNKS):
                hc = HOFF[s] + st
                x0 = XOFF[s] + st
                pz = ps()
                for q in range(2):
                    for kc in range(8):
                        k.mm(pz[:, q * 130:(q + 1) * 130], sl[:, kc, q * 128:(q + 1) * 128], Hany(kc, hc - 1, 130),
                             start=(kc == 0), stop=(kc == 7))
                shift(F["r"][:, :], pz[:, 0:130], 12)
                k.actf(LG[:, x0:x0 + 128], F["r"][:, :], AF.Sigmoid)
                shift(F["k"][:, :], pz[:, 130:260], 13)
                k.actf(LW[0:64, x0:x0 + 128], F["k"][0:64, :], AF.Tanh)
                k.copy(LW[64:128, x0:x0 + 128], F["k"][64:128, :], E=act)

            dbg = int(os.environ.get("K_RWDBG", "99"))

            def proj(ci, sl):
                s_, st_ = CHUNKS[ci]
                hc_ = HOFF[s_] + st_
                for q in range(3):
                    for kc in range(8):
                        k.mm(PS[0][:, q * 130:(q + 1) * 130], sl[:, kc, q * 128:(q + 1) * 128], Hany(kc, hc_ - 1, 130),
                             start=(kc == 0), stop=(kc == 7))

            def chunk_step(hp, d, s, ci, sl, nxt=None, par=0):
                TKM = TKM2[par]
                _, st = CHUNKS[ci]
                hc = HOFF[s] + st
                x0 = XOFF[s] + st
                S0, S0b = ST[s], STb[s]
                o = (lambda v: v) if d == 0 else rev
                last = 127 if d == 0 else 0
                r_, k_, v_, kkr, sq, nrm, kk, e_, a_, cum, cumx, E1, E2, E3, E4, t1, kd, ba = (
                    F[n_][:, :] for n_ in ["r", "k", "v", "kkr", "sq", "nrm", "kk", "e", "a", "cum", "cumx", "E1", "E2", "E3", "E4", "t1", "kd", "ba"])
                AT, BT, KTt, RT, BW, KW, vb = (B[n_][:, :] for n_ in ["AT", "BT", "KT", "RT", "BW", "KW", "vb"])
                pw = ps()
                k.mm(pw[:, 0:128], L2w[:, d, hp * 128:(hp + 1) * 128], LW[:, x0:x0 + 128])
                k.mm(pw[:, 128:256], L2a[:, d, hp * 128:(hp + 1) * 128], LW[:, x0:x0 + 128])
                pz = PS[0]
                k.softplus(e_, pw[:, 0:128], bias=nw0[:, d, hp:hp + 1], scale=-1.0, composite=True)
                k.actf(e_, e_, AF.Exp, bias=cc[:, 0:1], scale=-1.0)
                k.actf(a_, pw[:, 128:256], AF.Exp, bias=na0[:, d, hp:hp + 1], scale=-1.0)
                shift(r_, pz[:, 0:130], hp)
                shift(k_, pz[:, 130:260], 4 + hp)
                k.ts(kkr, k_, P("rw_kk", hp), ALU.mult)
                k.tt(sq, kkr, kkr, ALU.mult)
                pn = ps()
                k.mm(pn[:, 0:128], C("blk64"), sq)
                k.actf(nrm, pn[:, 0:128], AF.Ln)
                k.actf(nrm, nrm, AF.Exp, scale=-0.5)
                shift(v_, pz[:, 260:390], 8 + hp)
                if nxt is not None:
                    proj(nxt, sl)
                k.ts(a_, a_, 1.0, ALU.add)
                k.recip(a_, a_)
                ones_b = V(cc.t[:, 1:2].broadcast_to([128, 128]), cc)
                k.ts(e_, e_, -1.0, ALU.mult)
                k.scan(o(cum), ones_b, o(e_), 0.0, ALU.mult, ALU.add)
                k.ts(nrm, nrm, 1e12, ALU.min)
                k.tt(kk, kkr, nrm, ALU.mult)
                k.tt(cumx, cum, e_, ALU.subtract)
                k.actf(E3, cum, AF.Exp)
                k.copy(WL[:, par:par + 1], E3[:, last:last + 1], E=act)
                k.actf(E2, cum, AF.Exp, scale=-1.0)
                k.actf(E4, cum, AF.Exp, scale=-1.0, bias=cum[:, last:last + 1])
                k.actf(E1, cumx, AF.Exp)
                k.ts(t1, a_, -1.0, ALU.add, P("rw_ka", hp), ALU.mult)
                k.ts(t1, t1, 1.0, ALU.add)
                k.tt(kd, k_, t1, ALU.mult)
                k.tt(ba, kk, a_, ALU.mult)
                k.tt(RT, r_, E3, ALU.mult)
                k.tt(BT, ba, E2, ALU.mult)
                k.tt(KTt, kd, E2, ALU.mult)
                k.tt(BW, ba, E4, ALU.mult)
                k.tt(KW, kd, E4, ALU.mult)
                k.tt(AT, kk, E1, ALU.mult)
                k.copy(vb, v_, E=dve)
                k.stt(t1, r_, P("rw_rk", hp), kd, ALU.mult, ALU.mult)
                pbn = ps()
                k.mm(pbn[:, 0:128], C("blk64"), t1)
                if d == 0:
                    k.tt(BON[:, x0:x0 + 128], pbn[:, 0:128], v_, ALU.mult)
                else:
                    k.tt(sq, pbn[:, 0:128], v_, ALU.mult)
                    k.tt(BON[:, x0:x0 + 128], BON[:, x0:x0 + 128], sq, ALU.add)
                if dbg < 5:
                    return
                ptb = ps()
                p16 = psb16(ptb)
                for q, src in enumerate([AT, BW, KW, vb]):
                    k.tr(p16[:, q * 128:(q + 1) * 128], src, C("ident", True))
                k.copy(TKM[:, :, :].re("p a b -> p (a b)"), p16[:, 0:512], E=act)
                if dbg < 6:
                    return
                msk_s = C("lo_strict" if d == 0 else "up_strict")
                mskT_s = C("up_strict" if d == 0 else "lo_strict")
                mskT_i = C("up_incl" if d == 0 else "lo_incl")
                cur = [0, 0]

                def LMa(lv):
                    j = lv if d == 0 else 7 + lv
                    return lvm[:, j * 128:(j + 1) * 128]

                def LMb(lv):
                    j = 7 + lv if d == 0 else lv
                    return lvm[:, j * 128:(j + 1) * 128]
                p1s, p2s = [], []
                for hh in range(2):
                    PR = slice(hh * 64, hh * 64 + 64)
                    H_ = HB_[hh]
                    k.copy(H_["RTz"][par][PR, :], RT[PR, :], E=act)
                    k.copy(H_["ATz"][PR, :], AT[PR, :], E=act)
                for hh in range(2):
                    H_ = HB_[hh]
                    ATz, RTz = H_["ATz"][:, :], H_["RTz"][par][:, :]
                    p1 = ps()
                    p1s.append(p1)
                    k.mm(p1[:, 0:128], ATz, BT)
                    k.mm(p1[:, 128:256], BT, ATz)
                    k.mm(p1[:, 256:384], ATz, KTt)
                    p2 = ps()
                    p2s.append(p2)
                    k.mm(p2[:, 0:128], BT, RTz)
                    k.mm(p2[:, 128:256], KTt, RTz)
                for hh in range(2):
                    H_ = HB_[hh]
                    p1 = p1s[hh]
                    k.stt(H_["M0"][:, :], p1[:, 0:128], -1.0, msk_s, ALU.mult, ALU.mult)
                    k.stt(H_["MT0"][:, :], p1[:, 128:256], -1.0, mskT_s, ALU.mult, ALU.mult)
                    k.stt(H_["XT1"][:, :], p1[:, 0:128], -1.0, LMa(0), ALU.mult, ALU.mult)
                    k.tt(H_["XT1"][:, :], H_["XT1"][:, :], C("ident", True), ALU.add)
                    k.stt(H_["XT0"][:, :], p1[:, 128:256], -1.0, LMb(0), ALU.mult, ALU.mult)
                    k.tt(H_["XT0"][:, :], H_["XT0"][:, :], C("ident", True), ALU.add)
                for hh in range(2):
                    H_ = HB_[hh]
                    k.stt(H_["AAK"][:, :], p1s[hh][:, 256:384], -1.0, msk_s, ALU.mult, ALU.mult)
                    k.tt(H_["ARB"][par][:, :], p2s[hh][:, 0:128], mskT_i, ALU.mult)
                    k.tt(H_["ARK"][par][:, :], p2s[hh][:, 128:256], mskT_i, ALU.mult)
                if dbg < 7:
                    return (lambda: None)
                for lv in range(1, 7):
                    pTs, pXs = [], []
                    for hh in range(2):
                        H_ = HB_[hh]
                        pT = ps()
                        pTs.append(pT)
                        if lv < 6:
                            k.mm(pT[:, 0:128], H_["MT0"][:, :], H_["XT1"][:, :])
                        k.mm(pT[:, 128:256], H_["M0"][:, :], H_["XT0"][:, :])
                    for hh in range(2):
                        H_ = HB_[hh]
                        pT = pTs[hh]
                        if lv < 6:
                            k.tt(H_["M1"][:, :], pT[:, 0:128], LMa(lv), ALU.mult)
                        k.tt(H_["MT1"][:, :], pT[:, 128:256], LMb(lv), ALU.mult)
                    for hh in range(2):
                        H_ = HB_[hh]
                        pX = ps()
                        pXs.append(pX)
                        if lv < 6:
                            k.mm(pX[:, 0:128], C("ident", True), H_["XT1"][:, :], start=True, stop=False)
                            k.mm(pX[:, 0:128], H_["XT0"][:, :], H_["M1"][:, :], start=False, stop=True)
                        k.mm(pX[:, 128:256], C("ident", True), H_["XT0"][:, :], start=True, stop=False)
                        k.mm(pX[:, 128:256], H_["XT1"][:, :], H_["MT1"][:, :], start=False, stop=True)
                    for hh in range(2):
                        H_ = HB_[hh]
                        pX = pXs[hh]
                        k.copy(H_["XT0"][:, :], pX[:, 128:256], E=act)
                        if lv < 6:
                            k.copy(H_["XT1"][:, :], pX[:, 0:128], E=act)
                        cur[hh] = 0
                if dbg < 8:
                    return (lambda: None)
                PRs = [slice(0, 64), slice(64, 128)]
                Vks = [TKM[:, 3, 0:64], TKM[:, 3, 64:128]]
                Sbzs = [HB_[hh]["Sbz"][s][:, :] for hh in range(2)]
                pQs, pQ2s = [], []
                for hh in range(2):
                    H_ = HB_[hh]
                    XT = H_["XT0"][:, :]
                    pQ = ps()
                    pQs.append(pQ)
                    k.mm(pQ[:, 0:128], TKM[:, 0, :], XT)
                    pQ2 = ps()
                    pQ2s.append(pQ2)
                    k.mm(pQ2[:, 0:128], H_["AAK"][:, :], XT)
                for hh in range(2):
                    H_ = HB_[hh]
                    k.ts(H_["XAT"][par][PRs[hh], :], pQs[hh][PRs[hh], 0:128], -1.0, ALU.mult)
                    k.copy(H_["XAK"][par][:, :], pQ2s[hh][:, 0:128], E=act)

                def tail():
                    pUs, pYs, pSs = [], [], []
                    for hh in range(2):
                        H_ = HB_[hh]
                        pU = ps()
                        pUs.append(pU)
                        k.mm(pU[:, 0:64], H_["XAT"][par][:, :], Sbzs[hh], start=True, stop=False)
                        k.mm(pU[:, 0:64], H_["XAK"][par][:, :], Vks[hh], start=False, stop=True)
                    for hh in range(2):
                        k.copy(HB_[hh]["Ub"][:, :], pUs[hh][:, 0:64], E=(act if hh == 0 else dve))
                    for hh in range(2):
                        H_ = HB_[hh]
                        pS = ps()
                        pSs.append(pS)
                        k.mm(pS[:, 0:64], TKM[:, 1, :], H_["Ub"][:, :], start=True, stop=False)
                        k.mm(pS[:, 0:64], TKM[:, 2, :], Vks[hh], start=False, stop=True)
                        pY = ps()
                        pYs.append(pY)
                        k.mm(pY[:, 0:64], H_["RTz"][par][:, :], Sbzs[hh], start=True, stop=False)
                        k.mm(pY[:, 0:64], H_["ARB"][par][:, :], H_["Ub"][:, :], start=False, stop=False)
                        k.mm(pY[:, 0:64], H_["ARK"][par][:, :], Vks[hh], start=False, stop=True)
                    for hh in range(2):
                        PR = PRs[hh]
                        k.stt(S0[PR, 0:64], S0[PR, 0:64], WL[PR, par:par + 1], pSs[hh][PR, 0:64], ALU.mult, ALU.add)
                        k.copy(Sbzs[hh][PR, :], S0[PR, 0:64], E=act)
                    for hh in range(2):
                        pb = hh * 64
                        if d == 0:
                            k.copy(YA[:, ci, pb:pb + 64], pYs[hh][:, 0:64], E=act)
                        else:
                            k.tt(YA[:, ci, pb:pb + 64], YA[:, ci, pb:pb + 64], pYs[hh][:, 0:64], ALU.add)
                return tail

            small = bool(os.environ.get("K_RWSMALL"))
            cfg = os.environ.get("K_RWCFG")
            if cfg:
                c_hps, c_ds, c_ss = [[int(x) for x in part.split(",")] for part in cfg.split(";")]
            else:
                c_hps = list(range(1 if (dbg < 50 or small) else 4))
                c_ds = list(range(2 if dbg >= 50 else 1))
                c_ss = list(range(1 if (dbg < 50 or small) else 3))
            for hp in c_hps:
                if dbg < 1:
                    break
                sl = wslot()
                for q in range(3):
                    load_w(sl, d_odin, 0, 8, q * 512 + hp * 128, 128, q * 128)
                steps = []
                for d in c_ds:
                    for s in c_ss:
                        cis = [ci for ci, (s_, st_) in enumerate(CHUNKS) if s_ == s]
                        if d == 1:
                            cis = cis[::-1]
                        for j_, ci in enumerate(cis):
                            steps.append((d, s, ci, j_ == 0, j_ == len(cis) - 1))
                ps_reserve[0] = True
                proj(steps[0][2], sl)

                def runA(si_):
                    d_, s_, ci_, _, _ = steps[si_]
                    nx_ = steps[si_ + 1][2] if si_ + 1 < len(steps) else None
                    return chunk_step(hp, d_, s_, ci_, sl, nx_, si_ % 2)
                tail_cur = runA(0)
                for si_, (d, s, ci, first_, last_) in enumerate(steps):
                    tail_nxt = runA(si_ + 1) if si_ + 1 < len(steps) else None
                    S0 = ST[s]
                    if first_:
                        if s == 2 and not os.environ.get("K_RWNOSTATE"):
                            k.memset(stg[:, :], 0.0)
                            for hh in range(2):
                                k.dma(sp, stg[0:64, hh * 64:(hh + 1) * 64], d_strw[d, 2 * hp + hh, :, :])
                            pt = ps()
                            k.tr(pt[:, 0:128], stg[:, :], C("ident"))
                            k.copy(S0[:, 0:64], pt[:, 0:64])
                        else:
                            k.memset(S0[:, 0:64], 0.0)
                        for hh in range(2):
                            k.copy(HB_[hh]["Sbz"][s][hh * 64:(hh + 1) * 64, :], S0[hh * 64:(hh + 1) * 64, 0:64], E=act)
                    tail_cur()
                    tail_cur = tail_nxt
                    if last_ and s < 2:
                        pt = ps()
                        k.tr(pt[:, 0:128], S0[:, :], C("ident"))
                        k.copy(sto[0:64, :], pt[0:64, 0:128])
                        for hh in range(2):
                            k.dma(sp, o_srw[s, d, 2 * hp + hh, :, :], sto[0:64, hh * 64:(hh + 1) * 64])
                ps_reserve[0] = False
                ynP = [yn, F["r"]]
                t1P = [F["t1"], F["k"]]
                mvP = [mv, F["v"]]
                s6P = [st6, F["kk"]]

                def finX(ci, par):
                    for hh in range(2):
                        mvp = mvP[par]
                        bn(mvp[:, hh * 2:hh * 2 + 2], YA[:, ci, hh * 64:(hh + 1) * 64], s6P[par][:, 0:6])
                        k.actf(mvp[:, hh * 2 + 1:hh * 2 + 2], mvp[:, hh * 2 + 1:hh * 2 + 2], AF.Ln, bias=cc[:, 2:3])
                        k.actf(mvp[:, hh * 2 + 1:hh * 2 + 2], mvp[:, hh * 2 + 1:hh * 2 + 2], AF.Exp, scale=-0.5)
                        k.ts(ynP[par][:, hh * 64:(hh + 1) * 64], YA[:, ci, hh * 64:(hh + 1) * 64], mvp[:, hh * 2:hh * 2 + 1],
                             ALU.subtract, mvp[:, hh * 2 + 1:hh * 2 + 2], ALU.mult)

                def finY(ci, par):
                    s, st = CHUNKS[ci]
                    x0 = XOFF[s] + st
                    pT = ps()
                    k.tr(pT[:, 0:128], ynP[par][:, :], C("ident"))
                    pg = ps()
                    k.mm(pg[:, 0:128], G2[:, hp * 128:(hp + 1) * 128], LG[:, x0:x0 + 128])
                    t1 = t1P[par][:, :]
                    k.ts(t1, pT[:, 0:128], P("rw_lnw", hp), ALU.mult, P("rw_lnb", hp), ALU.add)
                    k.tt(t1, t1, BON[:, x0:x0 + 128], ALU.add)
                    k.tt(ymix[:, x0:x0 + 128], t1, pg[:, 0:128], ALU.mult)

                finX(0, 0)
                for ci in range(len(CHUNKS)):
                    if ci + 1 < len(CHUNKS):
                        finX(ci + 1, (ci + 1) % 2)
                    finY(ci, ci % 2)
                out_proj(d_odout, hp, ymix_tile, wout, l)

        norm_mod(0, 0)
        if stage >= 1 and stage != 66:
            layer0_mixers()
        if stage != 66:
            norm_mod(0, 1)
            mlp(0)
        if stage >= 5:
            if stage != 66:
                norm_mod(1, 0)
            if stage not in (6, 66):
                layer1_mlstm()
            if stage in (6, 7, 9, 66):
                layer1_rwkv()
            if stage != 66:
                norm_mod(1, 1)
                mlp(1)

        final_norm = stage >= 9
        k.scopes.append(ExitStack())
        norm_alloc()
        ostg = [k.sb(f"ostg{i}", [128, D]) for i in range(2)]
        ftmp = [k.sb(f"ftmp{i}", [128, 128]) for i in range(8)]
        fi = 0
        for ti in range(NT):
            s, st, n = TILES[ti]
            rs = norm_tile(ti) if final_norm else None
            for b4 in range(n // 128):
                og = ostg[(fi) % 2]
                fi += 1
                for half in range(2):
                    pt = ps()
                    for q in range(4):
                        c = half * 4 + q
                        src = V(xT.t[:, c, XOFF[s] + st + b4 * 128:XOFF[s] + st + (b4 + 1) * 128], XB[c][ti])
                        if final_norm:
                            ft = ftmp[c]
                            k.stt(ft[:, :], src, P("fnw", c), rs[:, b4 * 128:(b4 + 1) * 128], ALU.mult, ALU.mult)
                            src = ft[:, :]
                        k.tr(pt[:, q * 128:(q + 1) * 128], src, C("ident"))
                    k.copy(og[:, half * 512:(half + 1) * 512], pt[:, :], E=(act if half == 0 else dve))
                row0 = XOFF[s] + st + b4 * 128
                if row0 < 512:
                    k.dma(sp, o_yp[row0:row0 + 128, :], og[:, :])
                else:
                    k.dma(sp, o_ys[row0 - 512:row0 - 512 + 128, :], og[:, :])
        k.finish(outs)
        print("nsem", k.nsem, "instr", {e.name: e.cnt for e in [k.pe, k.act, k.dve, k.pool]})
        while k.scopes:
            k.scopes.pop().close()
    return nc


_CACHE = {}


def kernel(**inp):
    stage = int(os.environ.get("K_STAGE", "9"))
    inp = {k_: np.asarray(v) for k_, v in inp.items()}
    packs = []
    for core in range(NCORES):
        P = pack_params(inp, core)
        P.add_rows("eps", np.full((128, 1), EPS, np.float32))
        packs.append(P)
    pidx = packs[0].idx
    npar = packs[0].n
    key = (npar, stage)
    if key not in _CACHE:
        _CACHE[key] = build(pidx, npar, stage)
    nc = _CACHE[key]
    shared = {
        "cst": CONST_ARR,
        "w_mod": np.ascontiguousarray(inp["w_mod"].reshape(2 * D, 6 * D)),
        "mlp_w1": np.ascontiguousarray(inp["mlp_w1"].reshape(2 * D, 4 * D)),
        "mlp_w2": np.ascontiguousarray(inp["mlp_w2"].reshape(2 * 4 * D, D)),
        "ev_w_in": np.ascontiguousarray(inp["ev_w_in"][0]),
        "ev_w_out": np.ascontiguousarray(inp["ev_w_out"][0]),
        "pool_w": np.ascontiguousarray(inp["pool_w"][0].reshape(512, 128)),
        "rg_wa": np.ascontiguousarray(inp["rg_wa"][0].reshape(1024, 64)),
        "rg_wx": np.ascontiguousarray(inp["rg_wx"][0].reshape(1024, 64)),
        "od_w_in": np.ascontiguousarray(inp["od_w_in"][0]),
        "od_w_out": np.ascontiguousarray(inp["od_w_out"][0]),
        "rw_w2": np.ascontiguousarray(inp["rw_w2"][0].reshape(128, 512)),
        "rw_a2": np.ascontiguousarray(inp["rw_a2"][0].reshape(128, 512)),
        "rw_g2": np.ascontiguousarray(inp["rw_g2"][0]),
    }
    in_maps = []
    for core in range(NCORES):
        m = dict(shared)
        m["xp"] = np.ascontiguousarray(inp["x_prompt"][2 * core:2 * core + 2].reshape(512, D))
        m["xs"] = np.ascontiguousarray(inp["x_sample"][core].reshape(2048, D))
        m["par"] = packs[core].build()
        m["st_rw"] = np.ascontiguousarray(inp["state_rwkv"][core, 0])
        m["st_C"] = np.ascontiguousarray(inp["state_mlstm_C"][core, 0])
        m["st_n"] = np.ascontiguousarray(inp["state_mlstm_n"][core, 0])
        in_maps.append(m)
    res = run_bass_kernel_spmd(nc, in_maps, core_ids=list(range(NCORES)))
    R = res.results
    y_prompt = np.concatenate([r["yp"].reshape(2, 256, D) for r in R], axis=0)
    y_sample = np.stack([r["ys"] for r in R], axis=0)
    s_rg = np.concatenate([r["s_rg"].reshape(2, 1, 2, 512) for r in R], axis=0)
    s_rw = np.concatenate([r["s_rw"].reshape(2, 1, 2, 8, 64, 64) for r in R], axis=0)
    s_C = np.concatenate([r["s_C"].reshape(2, 1, 2, 4, 128, 128) for r in R], axis=0)
    s_n = np.concatenate([r["s_n"].reshape(2, 1, 2, 4, 128) for r in R], axis=0)
    s_m = np.concatenate([r["s_m"].reshape(2, 1, 2, 4) for r in R], axis=0)
    return (y_prompt.astype(np.float32), y_sample.astype(np.float32), s_rg.astype(np.float32),
            s_rw.astype(np.float32), s_C.astype(np.float32), s_n.astype(np.float32), s_m.astype(np.float32))
```

```python
import os
import numpy as np
import concourse.bass as bass
import concourse.mybir as mybir
from concourse.bass_utils import run_bass_kernel_spmd
from contextlib import ExitStack

F32 = mybir.dt.float32
BF16 = mybir.dt.bfloat16
AF = mybir.ActivationFunctionType
SOFTPLUS = AF.Exp if os.environ.get("K_NOSP") else AF.Softplus
ALU = mybir.AluOpType
AX = mybir.AxisListType

NCORES = 8
D = 1024
NTOK = 2560
SEQ_LEN = [256, 256, 2048]
XOFF = [0, 256, 512]
HPAD = 2
HOFF = [2, 262, 522]
HTOT = 2572
TILES = [(0, 0, 256), (1, 0, 256), (2, 0, 512), (2, 512, 512), (2, 1024, 512), (2, 1536, 512)]
NT = len(TILES)
EPS = 1e-6


NAMES = {}


class Sem:
    def __init__(self, h):
        self.h = h
        self.cnt = 0


class Eng:
    def __init__(self, name, e, sem, is_pe=False):
        self.name, self.e, self.sem, self.is_pe = name, e, sem, is_pe
        self.cnt = 0
        self.seen = {}


class Buf:
    def __init__(self, t, name=""):
        self.t = t
        self.name = name
        self.w = None
        self.r = {}
        self.dsem = None
        self.dcnt = 0
        self.is_dram = False
        self.allw = {}

    def __getitem__(self, idx):
        return V(self.t[idx], self)


class V:
    def __init__(self, ap, buf):
        self.ap, self.buf = ap, buf

    def __getitem__(self, idx):
        return V(self.ap[idx], self.buf)

    def re(self, s, **kw):
        return V(self.ap.rearrange(s, **kw), self.buf)


class K:
    def __init__(self, nc, es):
        self.nc, self.es = nc, es
        self.nsem = 0
        self.free_sems = {}
        self.scopes = []
        self.allbufs = []
        self.free_sems = {}
        mk = self.newsem
        self.pe = Eng("pe", nc.tensor, mk("pe"), True)
        self.act = Eng("act", nc.scalar, mk("act"))
        self.dve = Eng("dve", nc.vector, mk("dve"))
        self.pool = Eng("pool", nc.gpsimd, mk("pool"))
        self.sp = Eng("sp", nc.sync, mk("sp"))
        self.rr = 0

    def newsem(self, name, kind="hw"):
        fl = self.free_sems.setdefault(kind, []) if isinstance(getattr(self, "free_sems", None), dict) else None
        if fl:
            return fl.pop()
        self.nsem += 1
        sm = Sem(self.es.enter_context(self.nc.semaphore(f"s_{name}_{self.nsem}")))
        sm.kind = kind
        return sm

    def sb(self, name, shape, dt=F32):
        self.uid = getattr(self, "uid", 0) + 1
        es = self.scopes[-1] if getattr(self, "scopes", None) else self.es
        t = es.enter_context(self.nc.sbuf_tensor(f"sb_{name}_{self.uid}", list(shape), dt))
        b = Buf(t, name)
        self.allbufs.append(b)
        NAMES[name] = f"sb_{name}_{self.uid}"
        return b

    def barrier(self):
        engs = [self.pe, self.act, self.dve, self.pool, self.sp]
        for E in engs:
            for X in engs:
                if X is E or X.cnt == 0:
                    continue
                if E.seen.get(X.sem, 0) < X.cnt:
                    E.e.wait_ge(X.sem.h, X.cnt)
                    E.seen[X.sem] = X.cnt
            for b in self.allbufs:
                if b.dsem is not None and E.seen.get(b.dsem, 0) < b.dsem.cnt:
                    E.e.wait_ge(b.dsem.h, b.dsem.cnt)
                    E.seen[b.dsem] = b.dsem.cnt

    def scope(self):
        k = self

        class _S:
            def __enter__(s_):
                st = ExitStack()
                st.__enter__()
                k.scopes.append(st)
                s_.nb = len(k.allbufs)

            def __exit__(s_, *a):
                if a[0] is not None:
                    return False
                k.barrier()
                for b in k.allbufs[s_.nb:]:
                    if b.dsem is not None:
                        k.free_sems.setdefault(b.dsem.kind, []).append(b.dsem)
                        b.dsem = None
                del k.allbufs[s_.nb:]
                st = k.scopes.pop()
                st.__exit__(None, None, None)
                return False
        return _S()

    def view(self, parent, name=""):
        return Buf(parent.t, name)

    def _waits(self, E, reads, writes, dma_sem=None):
        waits = {}

        def need(ev):
            if ev is None:
                return
            s, v = ev
            if waits.get(s, 0) < v:
                waits[s] = v

        for b in reads:
            need(b.w)
        for b in writes:
            if not (dma_sem is not None and b.w is not None and b.w[0] is dma_sem):
                need(b.w)
            for s, v in b.r.items():
                need((s, v))
        for s, v in waits.items():
            if s is E.sem and E.is_pe:
                continue
            if E.seen.get(s, 0) >= v:
                continue
            E.e.wait_ge(s.h, v)
            E.seen[s] = v

    def emit(self, E, reads, writes, fn):
        reads = [b for b in reads if b is not None]
        self._waits(E, reads, writes)
        inst = fn()
        E.cnt += 1
        inst.then_inc(E.sem.h, 1)
        ev = (E.sem, E.cnt)
        for b in writes:
            b.w = ev
            b.r = {}
        for b in reads:
            if b not in writes:
                b.r[E.sem] = E.cnt
        return inst

    def dma(self, Q, out, in_, semof=None, **kw):
        sbuf = semof if semof is not None else (in_.buf if out.buf.is_dram else out.buf)
        qkind = "sw" if Q is self.pool else "hw"
        if sbuf.dsem is None:
            sbuf.dsem = self.newsem("d" + sbuf.name, qkind)
        assert sbuf.dsem.kind == qkind, (sbuf.name, qkind)
        rd = [] if in_.buf.is_dram else [in_.buf]
        wr = [] if out.buf.is_dram else [out.buf]
        self._waits(Q, rd, wr, dma_sem=sbuf.dsem)
        inst = Q.e.dma_start(out=out.ap, in_=in_.ap, **kw)
        sbuf.dsem.cnt += 16
        inst.then_inc(sbuf.dsem.h, 16)
        ev = (sbuf.dsem, sbuf.dsem.cnt)
        if out.buf.is_dram:
            out.buf.allw[sbuf.dsem] = sbuf.dsem.cnt
        else:
            out.buf.w = ev
            out.buf.r = {}
        if not in_.buf.is_dram:
            in_.buf.r[sbuf.dsem] = sbuf.dsem.cnt

    def _bufs(self, *vs):
        return [v.buf for v in vs if isinstance(v, V)]

    @staticmethod
    def _a(v):
        return v.ap if isinstance(v, V) else v

    def mm(self, out, lhsT, rhs, start=True, stop=True):
        return self.emit(self.pe, [lhsT.buf, rhs.buf], [out.buf],
                         lambda: self.nc.tensor.matmul(out.ap, lhsT.ap, rhs.ap, start=start, stop=stop))

    def tr(self, out, in_, ident):
        return self.emit(self.pe, [in_.buf, ident.buf], [out.buf],
                         lambda: self.nc.tensor.transpose(out.ap, in_.ap, ident.ap))

    def actf(self, out, in_, func, bias=None, scale=None, accum=None, E=None):
        kw = {}
        if bias is not None:
            kw["bias"] = self._a(bias)
        if scale is not None:
            kw["scale"] = self._a(scale)
        if accum is not None:
            kw["accum_out"] = accum.ap
        wr = [out.buf] + ([accum.buf] if accum is not None else [])
        return self.emit(self.act, self._bufs(in_, bias, scale), wr,
                         lambda: self.nc.scalar.activation(out=out.ap, in_=in_.ap, func=func, **kw))

    def softplus(self, out, in_, bias=None, scale=None, composite=False):
        if os.environ.get("K_NOSP") or composite:
            self.actf(out, in_, AF.Exp, bias=bias, scale=scale)
            return self.actf(out, out, AF.Ln, bias=self.onecol[0:out.ap.shape[0], :])
        return self.actf(out, in_, AF.Softplus, bias=bias, scale=scale)

    def _ve(self, E):
        return E if E is not None else self.dve

    def tt(self, out, in0, in1, op, E=None):
        E = self._ve(E)
        return self.emit(E, [in0.buf, in1.buf], [out.buf],
                         lambda: E.e.tensor_tensor(out=out.ap, in0=in0.ap, in1=in1.ap, op=op))

    def ts(self, out, in0, s1, op0, s2=None, op1=None, E=None):
        E = self._ve(E)
        kw = {}
        if op1 is not None:
            kw["op1"] = op1
        return self.emit(E, self._bufs(in0, s1, s2), [out.buf],
                         lambda: E.e.tensor_scalar(out=out.ap, in0=in0.ap, scalar1=self._a(s1), scalar2=self._a(s2),
                                                   op0=op0, **kw))

    def stt(self, out, in0, scalar, in1, op0, op1):
        return self.emit(self.dve, self._bufs(in0, scalar, in1), [out.buf],
                         lambda: self.nc.vector.scalar_tensor_tensor(out=out.ap, in0=in0.ap, scalar=self._a(scalar),
                                                                     in1=in1.ap, op0=op0, op1=op1))

    def scan(self, out, d0, d1, init, op0, op1):
        return self.emit(self.dve, self._bufs(d0, d1, init), [out.buf],
                         lambda: self.nc.vector.tensor_tensor_scan(out=out.ap, data0=d0.ap, data1=d1.ap,
                                                                   initial=self._a(init), op0=op0, op1=op1))

    def copy(self, out, in_, E=None):
        E = self._ve(E)
        if E is self.act:
            return self.emit(E, [in_.buf], [out.buf], lambda: self.nc.scalar.copy(out=out.ap, in_=in_.ap))
        return self.emit(E, [in_.buf], [out.buf], lambda: E.e.tensor_copy(out=out.ap, in_=in_.ap))

    def memset(self, out, val, E=None):
        E = self._ve(E)
        return self.emit(E, [], [out.buf], lambda: E.e.memset(out.ap, val))

    def recip(self, out, in_):
        return self.emit(self.dve, [in_.buf], [out.buf], lambda: self.nc.vector.reciprocal(out=out.ap, in_=in_.ap))

    def anyE(self):
        self.rr += 1
        return self.dve if self.rr % 2 else self.pool

    def finish(self, bufs):
        waits = {}
        for b in bufs:
            for s, v in b.allw.items():
                waits[s] = max(waits.get(s, 0), s.cnt)
        for s, v in waits.items():
            self.nc.sync.wait_ge(s.h, v)


class Pack:
    def __init__(self):
        self.cols = []
        self.idx = {}
        self.n = 0

    def add(self, name, vec):
        vec = np.asarray(vec, np.float32).reshape(-1)
        k = vec.size // 128
        assert k * 128 == vec.size, name
        self.idx[name] = (self.n, k)
        self.cols.append(vec.reshape(k, 128).T)
        self.n += k

    def add_rows(self, name, arr):
        arr = np.asarray(arr, np.float32)
        self.idx[name] = (self.n, arr.shape[1])
        self.cols.append(arr)
        self.n += arr.shape[1]

    def build(self):
        return np.ascontiguousarray(np.concatenate(self.cols, axis=1))


def make_consts():
    c = {}
    c["ident"] = np.eye(128, dtype=np.float32)
    i = np.arange(128)
    c["blk64"] = ((i[:, None] // 64) == (i[None, :] // 64)).astype(np.float32)
    c["ones"] = np.ones((128, 128), np.float32)
    c["up_incl"] = (i[:, None] <= i[None, :]).astype(np.float32)
    c["up_strict"] = (i[:, None] < i[None, :]).astype(np.float32)
    c["lo_incl"] = (i[:, None] >= i[None, :]).astype(np.float32)
    c["lo_strict"] = (i[:, None] > i[None, :]).astype(np.float32)
    for j in range(8):
        sel = np.zeros((128, 128), np.float32)
        sel[(j // 4) * 32 + (j % 4), :] = 1.0
        c[f"sel{j}"] = sel
    for lv in range(7):
        b = 1 << lv
        m = ((i[:, None] // (2 * b)) == (i[None, :] // (2 * b))) & ((i[:, None] // b) != (i[None, :] // b)) & (i[:, None] > i[None, :])
        c[f"lvL{lv}"] = m.astype(np.float32)
    for lv in range(7):
        c[f"lvU{lv}"] = np.ascontiguousarray(c[f"lvL{lv}"].T)
    names = list(c.keys())
    arr = np.concatenate([c[n] for n in names], axis=1)
    return names, np.ascontiguousarray(arr)


CONST_NAMES, CONST_ARR = make_consts()


def pack_params(inp, core):
    P = Pack()
    b = core
    P.add("c_ctx", inp["c_ctx"])
    P.add("c_s", inp["c"][b])
    for l in range(2):
        P.add(f"n1w{l}", inp["norm1_w"][l])
        P.add(f"n2w{l}", inp["norm2_w"][l])
        P.add(f"bmod{l}", inp["b_mod"][l])
    P.add("fnw", inp["final_norm_w"])
    P.add("pool_scale", inp["pool_scale"][0])
    for j in range(4):
        P.add(f"rg_cw{j}", inp["rg_conv_w"][0, j])
    P.add("rg_cb", inp["rg_conv_b"][0])
    for d in range(2):
        P.add(f"rg_ba{d}", inp["rg_ba"][0, d])
        P.add(f"rg_bx{d}", inp["rg_bx"][0, d])
        P.add(f"rg_lam{d}", inp["rg_lam"][0, d])
        P.add(f"rg_st{d}", inp["state_rglru"][b, 0, d])
    for j in range(4):
        P.add(f"ml_cw{j}", inp["ml_conv_w"][0, j])
    P.add("ml_cb", inp["ml_conv_b"][0])
    P.add("ml_nw", inp["ml_norm_w"][0])
    bi = np.zeros((128, 1), np.float32); bf = np.zeros((128, 1), np.float32); m0 = np.zeros((128, 1), np.float32)
    for d in range(2):
        for h in range(4):
            bi[d * 32 + h, 0] = inp["ml_bi"][0, d, h]
            bf[d * 32 + h, 0] = inp["ml_bf"][0, d, h]
            m0[d * 32 + h, 0] = inp["state_mlstm_m"][b, 0, d, h]
    P.add_rows("ml_bi", bi); P.add_rows("ml_bf", bf); P.add_rows("ml_m0", m0)
    P.add("rw_mu0", inp["rw_mu"][0, 0]); P.add("rw_mu1", inp["rw_mu"][0, 1])
    for d in range(2):
        P.add(f"rw_w0{d}", inp["rw_w0"][0, d]); P.add(f"rw_a0{d}", inp["rw_a0"][0, d])
    P.add("rw_kk", inp["rw_kk"][0]); P.add("rw_ka", inp["rw_ka"][0]); P.add("rw_rk", inp["rw_rk"][0].reshape(-1))
    P.add("rw_lnw", inp["rw_ln_w"][0]); P.add("rw_lnb", inp["rw_ln_b"][0])
    return P


def build(pidx, npar, stage):
    nc = bass.Bass("TRN2", target_bir_lowering=False)
    dram = {}

    def din(name, shape, dt=F32):
        dram[name] = Buf(nc.dram_tensor(name, list(shape), dt, kind="ExternalInput").ap(), name)
        dram[name].is_dram = True
        return dram[name]

    def dout(name, shape):
        dram[name] = Buf(nc.dram_tensor(name, list(shape), F32, kind="ExternalOutput").ap(), name)
        dram[name].is_dram = True
        return dram[name]

    d_xp = din("xp", [512, D])
    d_xs = din("xs", [2048, D])
    d_par = din("par", [128, npar])
    d_cst = din("cst", list(CONST_ARR.shape))
    d_wmod = din("w_mod", [2 * D, 6 * D])
    d_w1 = din("mlp_w1", [2 * D, 4 * D])
    d_w2 = din("mlp_w2", [2 * 4 * D, D])
    d_evin = din("ev_w_in", [D, 1536])
    d_evout = din("ev_w_out", [D, D])
    d_poolw = din("pool_w", [512, 128])
    d_rgwa = din("rg_wa", [1024, 64])
    d_rgwx = din("rg_wx", [1024, 64])
    d_odin = din("od_w_in", [D, 3856])
    d_odout = din("od_w_out", [D, D])
    d_rww2 = din("rw_w2", [128, 512])
    d_rwa2 = din("rw_a2", [128, 512])
    d_rwg2 = din("rw_g2", [128, 512])
    d_strw = din("st_rw", [2, 8, 64, 64])
    d_stC = din("st_C", [2, 4, 128, 128])
    d_stn = din("st_n", [2, 4, 128])
    o_yp = dout("yp", [512, D])
    o_ys = dout("ys", [2048, D])
    o_srg = dout("s_rg", [2, 2, 512])
    o_srw = dout("s_rw", [2, 2, 8, 64, 64])
    o_sC = dout("s_C", [2, 2, 4, 128, 128])
    o_sn = dout("s_n", [2, 2, 4, 128])
    o_sm = dout("s_m", [2, 2, 4])
    outs = [o_yp, o_ys, o_srg, o_srw, o_sC, o_sn, o_sm]

    with ExitStack() as es:
        k = K(nc, es)
        pe, act, dve, pool, sp = k.pe, k.act, k.dve, k.pool, k.sp
        k.allbufs.extend(dram.values())

        par = k.sb("par", [128, npar])
        cst = k.sb("cst", [128, 7 * 128])
        cstb = k.sb("cstb", [128, 384], BF16)
        xT = k.sb("xT", [128, 8, NTOK])
        hT = k.sb("hT", [128, 8, HTOT], BF16)
        XB = [[k.view(xT, f"x{c}_{t}") for t in range(NT)] for c in range(8)]
        HB = [k.view(hT, f"h{t}") for t in range(NT)]
        PS = [Buf(es.enter_context(nc.psum_tensor(f"ps{i}", [128, 512], F32)), f"ps{i}") for i in range(8)]
        psi = [0]

        ps_reserve = [False]

        def ps():
            psi[0] = (psi[0] + 1) % 8
            if ps_reserve[0] and psi[0] == 0:
                psi[0] = 1
            return PS[psi[0]]

        def P(name, j=0, n=1):
            o, kk = pidx[name]
            return par[:, o + j:o + j + n]

        def C(name, bf=False):
            i = CONST_NAMES.index(name)
            return (cstb if bf else cst)[:, i * 128:(i + 1) * 128]

        def Xv(c, ti):
            s, st, n = TILES[ti]
            return XB[c][ti][:, c, XOFF[s] + st:XOFF[s] + st + n]

        def Hv(c, ti):
            s, st, n = TILES[ti]
            return HB[ti][:, c, HOFF[s] + st:HOFF[s] + st + n]

        def Hany(c, col0, n):
            return V(hT.t[:, c, col0:col0 + n], HB[0])

        def h_reads():
            return HB

        onecol_b = k.sb("onecol", [128, 1])
        k.memset(onecol_b[:, :], 1.0)
        k.onecol = onecol_b[:, :]
        k.dma(sp, par[:, :], d_par[:, :])
        k.dma(sp, cst[:, :], d_cst[:, 0:7 * 128])
        k.copy(cstb[:, :], cst[:, 0:384])
        k.memset(hT[:, :, :].re("p a b -> p (a b)"), 0.0, E=pool)
        for t in range(1, NT):
            HB[t].w = HB[0].w

        WSL = [k.sb(f"wsl{i}", [128, 8, 512], BF16) for i in range(2)]
        wsi = [0]

        def wslot():
            wsi[0] ^= 1
            return WSL[wsi[0]]

        def load_w(slot, dW, r0, kc, c0, n, dst_c0=0):
            src = dW.t[r0:r0 + kc * 128, c0:c0 + n].rearrange("(k p) n -> p k n", p=128)
            k.dma(pool, slot[:, 0:kc, dst_c0:dst_c0 + n], V(src, dW))

        scT = k.sb("scT", [128, 8, 2], BF16)
        k.actf(scT[:, :, 0], P("c_ctx", 0, 8), AF.Silu)
        k.actf(scT[:, :, 1], P("c_s", 0, 8), AF.Silu)
        modt = k.sb("modt", [128, 2, 48, 2])
        for l in range(2):
            pm = ps()
            for g in range(12):
                sl = wslot()
                load_w(sl, d_wmod, l * D, 8, g * 512, 512)
                for f in range(4):
                    j = g * 4 + f
                    for kc in range(8):
                        k.mm(pm[:, j * 2:j * 2 + 2], sl[:, kc, f * 128:(f + 1) * 128], scT[:, kc, :],
                             start=(kc == 0), stop=(kc == 7))
            o, _ = pidx[f"bmod{l}"]
            k.tt(modt[:, l, :, :], pm[:, 0:96].re("p (j c) -> p j c", c=2),
                 V(par.t[:, o:o + 48].unsqueeze(2).broadcast_to([128, 48, 2]), par), ALU.add)
        modA = k.sb("modA", [128, 2, 2, 8, 2])
        for l in range(2):
            for w_, (nw, scj) in enumerate([(f"n1w{l}", 8), (f"n2w{l}", 32)]):
                o, _ = pidx[nw]
                nwb = V(par.t[:, o:o + 8].unsqueeze(2).broadcast_to([128, 8, 2]), par)
                k.stt(modA[:, l, w_, :, :], modt[:, l, scj:scj + 8, :], 1.0, nwb, ALU.add, ALU.mult)

        def modB(l, w_, c, cond):
            j = (0 if w_ == 0 else 24) + c
            return modt[:, l, j, cond:cond + 1]

        def modG(l, w_, c, cond):
            j = (16 if w_ == 0 else 40) + c
            return modt[:, l, j, cond:cond + 1]

        k.scopes.append(ExitStack())
        xin = [k.sb(f"xin{i}", [128, D]) for i in range(2)]
        for tb in range(20):
            src = d_xp if tb < 4 else d_xs
            r0 = tb * 128 if tb < 4 else (tb - 4) * 128
            xi = xin[tb % 2]
            k.dma(sp, xi[:, :], src[r0:r0 + 128, :])
            col0 = tb * 128
            ti = [i for i, (s, st, n) in enumerate(TILES) if XOFF[s] + st <= col0 < XOFF[s] + st + n][0]
            for half in range(2):
                pt = ps()
                for q in range(4):
                    c = half * 4 + q
                    k.tr(pt[:, q * 128:(q + 1) * 128], xi[:, c * 128:(c + 1) * 128], C("ident"))
                bufs = [XB[half * 4 + q][ti] for q in range(4)]
                dst = V(xT.t[:, half * 4:half * 4 + 4, col0:col0 + 128], bufs[0])
                E = act if half == 0 else dve
                k._waits(E, [], bufs[1:])
                k.copy(dst, pt[:, :].re("p (q n) -> p q n", q=4), E=E)
                for bb in bufs[1:]:
                    bb.w = bufs[0].w
                    bb.r = {}

        k.barrier()
        k.scopes.pop().close()
        nb = {}
        cnt = {"sq": 0, "rs": 0, "nt": 0}

        def norm_alloc():
            nb["sqb"] = [k.sb(f"sqb{i}", [128, 512], BF16) for i in range(8)]
            nb["rstd"] = [k.sb(f"rstd{i}", [128, 512]) for i in range(2)]
            nb["ntmp"] = [k.sb(f"ntmp{i}", [128, 512]) for i in range(8)]

        def rot(lst, key):
            cnt[key] += 1
            return lst[cnt[key] % len(lst)]

        def norm_tile(ti):
            s, st, n = TILES[ti]
            pss = ps()
            for c in range(8):
                sq = rot(nb["sqb"], "sq")
                k.actf(sq[:, :n], Xv(c, ti), AF.Square)
                k.mm(pss[:, :n], C("ones", True), sq[:, :n], start=(c == 0), stop=(c == 7))
            rs = rot(nb["rstd"], "rs")
            k.actf(rs[:, :n], pss[:, :n], AF.Sqrt, bias=P("eps"), scale=1.0 / D)
            k.recip(rs[:, :n], rs[:, :n])
            return rs

        def norm_mod(l, w_):
          with k.scope():
            norm_alloc()
            for ti in range(NT):
                s, st, n = TILES[ti]
                cond = 0 if s < 2 else 1
                rs = norm_tile(ti)
                for c in range(8):
                    tmp = rot(nb["ntmp"], "nt")
                    k.tt(tmp[:, :n], Xv(c, ti), rs[:, :n], ALU.mult, E=k.anyE())
                    k.actf(Hv(c, ti), tmp[:, :n], AF.Identity, bias=modB(l, w_, c, cond),
                           scale=modA[:, l, w_, c, cond:cond + 1])

        def x_accum(c, ti, psv, l, w_):
            s, st, n = TILES[ti]
            cond = 0 if s < 2 else 1
            k.stt(Xv(c, ti), psv, modG(l, w_, c, cond), Xv(c, ti), ALU.mult, ALU.add)

        mc = {"h": 0, "r": 0}

        def mlp(l):
          with k.scope():
            hid = [k.sb(f"hid{i}", [128, 4, 512], BF16) for i in range(2)]
            w2s = [k.sb(f"w2s{i}", [128, 4, 1024], BF16) for i in range(2)]
            rl = [k.sb(f"rl{i}", [128, 512]) for i in range(2)]
            S1, S2, HBUF = {}, {}, {}

            def load(g):
                s1 = wslot()
                load_w(s1, d_w1, l * D, 8, g * 512, 512)
                s2 = w2s[g % 2]
                src = d_w2.t[l * 4 * D + g * 512:l * 4 * D + (g + 1) * 512, :].rearrange("(k p) n -> p k n", p=128)
                k.dma(pool, s2[:, :, :], V(src, d_w2))
                S1[g], S2[g] = s1, s2

            units = [(g, ti) for g in range(8) for ti in range(NT)]

            def mm1(u):
                g, ti = units[u]
                s, st, n = TILES[ti]
                hb = hid[u % 2]
                HBUF[u] = hb
                for j in range(4):
                    p1 = ps()
                    for kc in range(8):
                        k.mm(p1[:, :n], S1[g][:, kc, j * 128:(j + 1) * 128], Hv(kc, ti), start=(kc == 0), stop=(kc == 7))
                    mc["r"] += 1
                    r_ = rl[mc["r"] % 2]
                    k.actf(r_[:, :n], p1[:, :n], AF.Relu)
                    k.tt(hb[:, j, :n], r_[:, :n], r_[:, :n], ALU.mult)

            def mm2(u):
                g, ti = units[u]
                s, st, n = TILES[ti]
                hb = HBUF[u]
                for oc in range(8):
                    p2 = ps()
                    for j in range(4):
                        k.mm(p2[:, :n], S2[g][:, j, oc * 128:(oc + 1) * 128], hb[:, j, :n], start=(j == 0), stop=(j == 3))
                    x_accum(oc, ti, p2[:, :n], l, 1)
                if ti == NT - 1 and g + 2 < 8:
                    load(g + 2)

            load(0)
            load(1)
            mm1(0)
            for u in range(len(units)):
                if u + 1 < len(units):
                    mm1(u + 1)
                mm2(u)

        def out_proj(dW, chunk, yv_of_tile, wout, l):
            wo = wout[chunk % 2]
            k.dma(pool, wo[:, :], dW[chunk * 128:(chunk + 1) * 128, :])
            for ti in range(NT):
                s, st, n = TILES[ti]
                for oc in range(8):
                    p2 = ps()
                    k.mm(p2[:, :n], wo[:, oc * 128:(oc + 1) * 128], yv_of_tile(ti))
                    x_accum(oc, ti, p2[:, :n], l, 0)

        def box1d(src, nouter, L, unit, w, A, B):
            lo = w // 2
            Lp = L + w - 1
            tot = nouter * Lp * unit
            av = A[:, 0:tot].re("p (o l u) -> p o l u", o=nouter, l=Lp, u=unit)
            bv = B[:, 0:tot].re("p (o l u) -> p o l u", o=nouter, l=Lp, u=unit)
            k.memset(A[:, 0:tot], 0.0, E=pool)
            k.copy(av[:, :, lo:lo + L, :], src, E=pool)
            cur, oth = av, bv
            span = Lp
            sh = 1
            while sh < w:
                span -= sh
                k.tt(oth[:, :, 0:span, :], cur[:, :, 0:span, :], cur[:, :, sh:sh + span, :], ALU.add, E=k.anyE())
                cur, oth = oth, cur
                sh *= 2
            assert span == L
            return cur[:, :, 0:L, :], oth

        def v4(v, o, l, u):
            return v.re("p (o l u) -> p o l u", o=o, l=l, u=u)

        def layer0_pool():
          l = 0
          with k.scope():
            ymix = k.sb("ymix", [128, NTOK], BF16)
            wout = [k.sb(f"wout{i}", [128, D], BF16) for i in range(2)]

            def ymix_tile(ti):
                s, st, n = TILES[ti]
                return ymix[:, XOFF[s] + st:XOFF[s] + st + n]
            ubuf = k.sb("ubuf", [128, NTOK])
            M1 = k.sb("M1", [128, 2048])
            pa = k.sb("pa", [128, 1536])
            pb = k.sb("pb", [128, 1536])
            pw = k.sb("pw", [128, 128], BF16)
            dbf = k.sb("dbf", [128, NTOK], BF16)
            rc = k.sb("rc", [128, 3, 256])
            onesb = k.sb("onesb", [128, 256])
            k.memset(onesb[:, :], 1.0)
            for g in range(4):
                w = 2 << g
                for kind, L in enumerate([256, 32, 64]):
                    r, _ = box1d(v4(onesb[:, 0:L], 1, L, 1), 1, L, 1, w, pa, pb)
                    k.recip(v4(rc[:, kind, 0:L], 1, L, 1), r)
                sl = wslot()
                load_w(sl, d_evin, 0, 8, g * 128, 128)
                k.dma(pool, pw[:, :], d_poolw[g * 128:(g + 1) * 128, :])
                for ti in range(NT):
                    s, st, n = TILES[ti]
                    p1 = ps()
                    for kc in range(8):
                        k.mm(p1[:, :n], sl[:, kc, 0:128], Hv(kc, ti), start=(kc == 0), stop=(kc == 7))
                    k.copy(ubuf[:, XOFF[s] + st:XOFF[s] + st + n], p1[:, :n], E=act)
                for s in range(2):
                    uv = ubuf[:, XOFF[s]:XOFF[s] + 256]
                    r, oth = box1d(v4(uv, 1, 256, 1), 1, 256, 1, w, pa, pb)
                    m = oth[:, :, 0:256, :]
                    k.tt(m, r, v4(rc[:, 0, 0:256], 1, 256, 1), ALU.mult)
                    k.tt(v4(dbf[:, XOFF[s]:XOFF[s] + 256], 1, 256, 1), m, v4(uv, 1, 256, 1), ALU.subtract, E=pool)
                u3 = ubuf[:, 512:2560].re("p (l u) -> p l u", u=64)
                m13 = M1[:, :].re("p (l u) -> p l u", u=64)
                for hf in range(2):
                    src = u3[:, :, hf * 32:(hf + 1) * 32]
                    r, _ = box1d(V(src.ap.unsqueeze(1), src.buf), 1, 32, 32, w, pa, pb)
                    rcr = V(rc.t[:, 1, 0:32].unsqueeze(2).broadcast_to([128, 32, 32]), rc)
                    k.tt(m13[:, :, hf * 32:(hf + 1) * 32], r[:, 0, :, :], rcr, ALU.mult)
                d3 = dbf[:, 512:2560].re("p (l u) -> p l u", u=64)
                for hf in range(2):
                    src = m13[:, hf * 16:(hf + 1) * 16, :]
                    r2, oth = box1d(V(src.ap.unsqueeze(3), src.buf), 16, 64, 1, w, pa, pb)
                    rcc = V(rc.t[:, 2, 0:64].unsqueeze(1).broadcast_to([128, 16, 64]), rc)
                    m2 = oth[:, :, 0:64, 0]
                    k.tt(m2, r2[:, :, :, 0], rcc, ALU.mult)
                    k.tt(d3[:, hf * 16:(hf + 1) * 16, :], m2, u3[:, hf * 16:(hf + 1) * 16, :], ALU.subtract, E=pool)
                for ti in range(NT):
                    s, st, n = TILES[ti]
                    p1 = ps()
                    k.mm(p1[:, :n], pw[:, :], dbf[:, XOFF[s] + st:XOFF[s] + st + n])
                    k.actf(ymix_tile(ti), p1[:, :n], AF.Copy, scale=P("pool_scale", g))
                out_proj(d_evout, g, ymix_tile, wout, l)

        def layer0_rg():
          l = 0
          with k.scope():
            ymix = k.sb("ymix", [128, NTOK], BF16)
            wout = [k.sb(f"wout{i}", [128, D], BF16) for i in range(2)]

            def ymix_tile(ti):
                s, st, n = TILES[ti]
                return ymix[:, XOFF[s] + st:XOFF[s] + st + n]
            UP = k.sb("UP", [128, HTOT])
            HF = k.sb("HF", [128, NTOK])
            GEL = k.sb("GEL", [128, NTOK])
            wblk = k.sb("wblk", [128, 4, 128], BF16)
            wstg = k.sb("wstg", [128, 4, 64])
            nsp = k.sb("nsp", [128, 4])
            srg = k.sb("srg", [128, 2, 2, 4])
            seg = {n_: k.sb(f"sg_{n_}", [128, 512]) for n_ in ["xc", "r", "i", "s", "hb"]}
            xb = k.sb("xcb", [128, 512], BF16)
            RGT = [(seg["xc"], seg["r"], seg["i"], seg["s"], xb),
                   (k.sb("sg_xc1", [128, 512]), k.sb("sg_r1", [128, 512]), k.sb("sg_i1", [128, 512]), k.sb("sg_s1", [128, 512]),
                    k.sb("xcb1", [128, 512], BF16))]
            k.memset(UP[:, :], 0.0, E=pool)
            k.memset(wblk[:, :, :].re("p a b -> p (a b)"), 0.0, E=pool)
            carry = k.sb("carry", [128, 4])
            for c in range(4):
                sl = wslot()
                load_w(sl, d_evin, 0, 8, 512 + c * 128, 128, 0)
                load_w(sl, d_evin, 0, 8, 1024 + c * 128, 128, 128)
                for d in range(2):
                    for gi, dWg in enumerate([d_rgwa, d_rgwx]):
                        r0 = (d * 8 + 2 * c) * 64
                        k.dma(sp, wstg[:, d * 2 + gi, :], dWg[r0:r0 + 128, :])
                for d in range(2):
                    for gi in range(2):
                        k.copy(wblk[0:64, d * 2 + gi, 0:64], wstg[0:64, d * 2 + gi, :])
                        k.copy(wblk[64:128, d * 2 + gi, 64:128], wstg[64:128, d * 2 + gi, :])
                    k.softplus(nsp[:, d * 2:d * 2 + 1], P(f"rg_lam{d}", c), scale=-1.0)
                    k.ts(nsp[:, d * 2:d * 2 + 1], nsp[:, d * 2:d * 2 + 1], -8.0, ALU.mult)
                    k.ts(nsp[:, d * 2 + 1:d * 2 + 2], nsp[:, d * 2:d * 2 + 1], 2.0, ALU.mult)
                for ti in range(NT):
                    s, st, n = TILES[ti]
                    p1 = ps()
                    p2 = ps()
                    for kc in range(8):
                        k.mm(p1[:, :n], sl[:, kc, 0:128], Hv(kc, ti), start=(kc == 0), stop=(kc == 7))
                    for kc in range(8):
                        k.mm(p2[:, :n], sl[:, kc, 128:256], Hv(kc, ti), start=(kc == 0), stop=(kc == 7))
                    k.copy(UP[:, HOFF[s] + st:HOFF[s] + st + n], p1[:, :n], E=act)
                    k.actf(GEL[:, XOFF[s] + st:XOFF[s] + st + n], p2[:, :n], AF.Gelu_apprx_tanh)
                def rgA(d, ti, T):
                    xc, r_, i_, s_, xb_ = T
                    s, st, n = TILES[ti]
                    h0 = HOFF[s] + st
                    k.ts(xc[:, :n], UP[:, h0 - 2:h0 - 2 + n], P("rg_cw0", c), ALU.mult, P("rg_cb", c), ALU.add)
                    for j in range(1, 4):
                        k.stt(xc[:, :n], UP[:, h0 - 2 + j:h0 - 2 + j + n], P(f"rg_cw{j}", c), xc[:, :n], ALU.mult, ALU.add)
                    k.copy(xb_[:, :n], xc[:, :n], E=act)
                    pr = ps()
                    pi = ps()
                    k.mm(pr[:, :n], wblk[:, d * 2, :], xb_[:, :n])
                    k.mm(pi[:, :n], wblk[:, d * 2 + 1, :], xb_[:, :n])
                    k.actf(r_[:, :n], pr[:, :n], AF.Sigmoid, bias=P(f"rg_ba{d}", c))
                    k.actf(i_[:, :n], pi[:, :n], AF.Sigmoid, bias=P(f"rg_bx{d}", c))
                    k.actf(s_[:, :n], r_[:, :n], AF.Exp, scale=nsp[:, d * 2 + 1:d * 2 + 2])
                    k.actf(r_[:, :n], r_[:, :n], AF.Exp, scale=nsp[:, d * 2:d * 2 + 1])
                    k.tt(i_[:, :n], i_[:, :n], xc[:, :n], ALU.mult)
                    k.ts(s_[:, :n], s_[:, :n], -1.0, ALU.mult, 1.0, ALU.add)
                    k.ts(s_[:, :n], s_[:, :n], 0.0, ALU.max)
                    k.actf(s_[:, :n], s_[:, :n], AF.Sqrt)
                    k.tt(i_[:, :n], i_[:, :n], s_[:, :n], ALU.mult)

                def rgB(d, ti, T):
                    xc, r_, i_, s_, xb_ = T
                    hb = seg["hb"]
                    s, st, n = TILES[ti]
                    first = (st == 0) if d == 0 else (st + n == SEQ_LEN[s])
                    if first:
                        init = P(f"rg_st{d}", c) if s == 2 else 0.0
                    else:
                        init = carry[:, d:d + 1]
                    x0 = XOFF[s] + st
                    if d == 0:
                        dst = HF[:, x0:x0 + n]
                        k.scan(dst, r_[:, :n], i_[:, :n], init, ALU.mult, ALU.add)
                        k.copy(carry[:, 0:1], HF[:, x0 + n - 1:x0 + n])
                        if s < 2 and st + n == SEQ_LEN[s]:
                            k.copy(srg[:, s, 0, c:c + 1], HF[:, x0 + n - 1:x0 + n])
                    else:
                        k.scan(V(hb.t[:, 0:n][:, ::-1], hb), V(r_.t[:, 0:n][:, ::-1], r_),
                               V(i_.t[:, 0:n][:, ::-1], i_), init, ALU.mult, ALU.add)
                        k.copy(carry[:, 1:2], hb[:, 0:1])
                        if s < 2 and st == 0:
                            k.copy(srg[:, s, 1, c:c + 1], hb[:, 0:1])
                        k.tt(hb[:, :n], hb[:, :n], HF[:, x0:x0 + n], ALU.add)
                        k.tt(ymix[:, x0:x0 + n], hb[:, :n], GEL[:, x0:x0 + n], ALU.mult)

                for d in range(2):
                    order = list(range(NT)) if d == 0 else list(range(NT - 1, -1, -1))
                    rgA(d, order[0], RGT[0])
                    for j_, ti in enumerate(order):
                        if j_ + 1 < len(order):
                            rgA(d, order[j_ + 1], RGT[(j_ + 1) % 2])
                        rgB(d, ti, RGT[j_ % 2])
                out_proj(d_evout, 4 + c, ymix_tile, wout, l)
            for s in range(2):
                for d in range(2):
                    dst = o_srg.t[s, d, :].rearrange("(c p) -> p c", p=128)
                    k.dma(sp, V(dst, o_srg), srg[:, s, d, :], allow_slow_non_contiguous=True)

        def layer0_mixers():
            layer0_pool()
            layer0_rg()

        CHUNKS = [(s_, st_) for s_ in range(3) for st_ in range(0, SEQ_LEN[s_], 128)]

        def psb16(buf):
            return V(buf.t[:, :].bitcast(BF16), buf)

        def bn(out_mv, in_, st6):
            k.emit(dve, [in_.buf], [st6.buf], lambda: nc.vector.bn_stats(out=st6.ap, in_=in_.ap))
            k.emit(dve, [st6.buf], [out_mv.buf], lambda: nc.vector.bn_aggr(out=out_mv.ap, in_=st6.ap))

        def rev(v):
            return V(v.ap[:, ::-1], v.buf)

        def layer1_mlstm():
          l = 1
          with k.scope():
            ymix = k.sb("ymix", [128, NTOK], BF16)
            wout = [k.sb(f"wout{i}", [128, D], BF16) for i in range(2)]

            def ymix_tile(ti):
                s, st, n = TILES[ti]
                return ymix[:, XOFF[s] + st:XOFF[s] + st + n]
            selt = k.sb("selt", [128, 8 * 128])
            k.dma(sp, selt[:, :], d_cst[:, 7 * 128:15 * 128])
            GR = k.sb("GR", [128, HTOT])
            TK = k.sb("TK", [128, 20, 3, 8])
            SMT = k.sb("SMT", [128, 2])
            onec = k.sb("onec", [128, 1])
            nbf = k.sb("nbf", [128, 1])
            k.memset(onec[:, :], 1.0)
            k.ts(nbf[:, :], P("ml_bf"), -1.0, ALU.mult)
            with k.scope():
                LI = k.sb("LI", [128, HTOT])
                LF = k.sb("LF", [128, HTOT])
                FF = k.sb("FF", [128, HTOT])
                gst = k.sb("gst", [128, 8, 16])
                sg = k.sb("sg", [128, 8, 2, 128], BF16)
                k.dma(sp, gst[:, :, :], V(d_odin.t[:, 3840:3856].rearrange("(k p) n -> p k n", p=128), d_odin))
                k.memset(sg[:, :, :, :].re("p a b c -> p (a b c)"), 0.0, E=pool)
                for d in range(2):
                    for gi in range(2):
                        k.copy(sg[:, :, gi, d * 32:d * 32 + 4], gst[:, :, d * 8 + gi * 4:d * 8 + gi * 4 + 4])
                k.memset(LI[:, :], 0.0, E=pool)
                k.memset(LF[:, :], 0.0, E=pool)
                k.memset(FF[:, :], 0.0, E=pool)
                k.memset(GR[:, :], 0.0, E=pool)
                for ti in range(NT):
                    s, st, n = TILES[ti]
                    pli = ps()
                    plf = ps()
                    for kc in range(8):
                        k.mm(pli[:, :n], sg[:, kc, 0, :], Hv(kc, ti), start=(kc == 0), stop=(kc == 7))
                    for kc in range(8):
                        k.mm(plf[:, :n], sg[:, kc, 1, :], Hv(kc, ti), start=(kc == 0), stop=(kc == 7))
                    hc = HOFF[s] + st
                    k.actf(LI[0:64, hc:hc + n], pli[0:64, :n], AF.Identity, bias=P("ml_bi")[0:64, :])
                    k.softplus(LF[0:64, hc:hc + n], plf[0:64, :n], bias=nbf[0:64, :], scale=-1.0)
                    k.ts(LF[0:64, hc:hc + n], LF[0:64, hc:hc + n], -1.0, ALU.mult, E=pool)
                for s in range(3):
                    n = SEQ_LEN[s]
                    hc = HOFF[s]
                    for d in range(2):
                        R_ = slice(d * 32, d * 32 + 4)
                        o = (lambda v: v) if d == 0 else rev
                        ones_b = V(onec.t[R_, 0:1].broadcast_to([4, n]), onec)
                        k.scan(o(FF[R_, hc:hc + n]), ones_b, o(LF[R_, hc:hc + n]), 0.0, ALU.mult, ALU.add)
                        k.tt(LI[R_, hc:hc + n], LI[R_, hc:hc + n], FF[R_, hc:hc + n], ALU.subtract)
                        init = P("ml_m0")[R_, :] if s == 2 else 0.0
                        k.scan(o(GR[R_, hc:hc + n]), o(LI[R_, hc:hc + n]), o(LI[R_, hc:hc + n]), init, ALU.max, ALU.max)
                        k.tt(FF[R_, hc:hc + n], FF[R_, hc:hc + n], GR[R_, hc:hc + n], ALU.add)
                        halo = hc - 1 if d == 0 else hc + n
                        if s == 2:
                            k.copy(GR[R_, halo:halo + 1], P("ml_m0")[R_, :])
                        if s < 2:
                            fin = hc + n - 1 if d == 0 else hc
                            k.copy(SMT[R_, s:s + 1], FF[R_, fin:fin + 1])
                for ci, (s, st) in enumerate(CHUNKS):
                    hc = HOFF[s] + st
                    pt = ps()
                    for q, RB in enumerate([LI, GR, FF]):
                        k.tr(pt[:, q * 128:(q + 1) * 128], RB[:, hc:hc + 128], C("ident"))
                    src = V(pt.t[:, 0:384].rearrange("p (q a b) -> p q a b", q=3, a=4)[:, :, 0:2, 0:4], pt)
                    k.copy(TK[:, ci, :, :].re("p q (a b) -> p q a b", a=2), src, E=act)
                for s in range(2):
                    for d in range(2):
                        k.dma(sp, V(o_sm.t[s, d, :].rearrange("(h o) -> h o", o=1), o_sm), SMT[d * 32:d * 32 + 4, s:s + 1])
            QT = k.sb("QT", [128, NTOK], BF16)
            KT_ = k.sb("KT", [128, NTOK], BF16)
            KTOK = k.sb("KTOK", [128, 20, 128], BF16)
            V1 = k.sb("V1", [128, 20, 130], BF16)
            HS = k.sb("HS", [128, 20, 128])
            cacc = k.sb("cacc", [128, 128])
            qtmp = k.sb("qtmp", [128, 128])
            TS = [(k.sb(f"Eb{i_}", [128, 128]), k.sb(f"Pb{i_}", [128, 128], BF16), k.sb(f"kw{i_}", [128, 128], BF16),
                   k.sb(f"cols{i_}", [128, 8])) for i_ in range(2)]
            T1 = k.sb("T1", [128, 130])
            ND = k.sb("ND", [128, 130])
            CN = [k.sb(f"CN{s}", [128, 130]) for s in range(3)]
            CNb = [k.sb(f"CNb{s}", [128, 130], BF16) for s in range(3)]
            st6 = k.sb("st6", [128, 6])
            mv = k.sb("mv", [128, 2])
            hn = k.sb("hn", [128, 128])
            sgo = k.sb("sgo", [128, 128])
            k.memset(V1[:, :, :].re("p a b -> p (a b)"), 1.0, E=pool)
            def ml_load(h_):
                sl_ = wslot()
                for q in range(4):
                    load_w(sl_, d_odin, 0, 8, 1792 + q * 512 + h_ * 128, 128, q * 128)
                return sl_
            sl_next = ml_load(0)
            for h in range(4):
                sl = sl_next
                sl_next = ml_load(h + 1) if h + 1 < 4 else None
                caq = [cacc, TS[0][0]]
                cak = [hn, TS[1][0]]
                qtm = [qtmp, sgo]

                def m1X(ci, p_):
                    s, st = CHUNKS[ci]
                    hc = HOFF[s] + st
                    x0 = XOFF[s] + st
                    pz = ps()
                    for q in range(2):
                        for kc in range(8):
                            k.mm(pz[:, q * 131:q * 131 + 131], sl[:, kc, q * 128:(q + 1) * 128], Hany(kc, hc - 2, 131),
                                 start=(kc == 0), stop=(kc == 7))
                    pv = ps()
                    for kc in range(8):
                        k.mm(pv[:, 0:128], Hany(kc, hc, 128), sl[:, kc, 256:384], start=(kc == 0), stop=(kc == 7))
                    for q in range(2):
                        cj = q * 4 + h
                        z = pz[:, q * 131:q * 131 + 131]
                        acc = (caq if q == 0 else cak)[p_]
                        k.ts(acc[:, :], z[:, 0:128], P("ml_cw0", cj), ALU.mult, P("ml_cb", cj), ALU.add)
                        for j in range(1, 4):
                            k.stt(acc[:, :], z[:, j:j + 128], P(f"ml_cw{j}", cj), acc[:, :], ALU.mult, ALU.add)
                        if q == 0:
                            k.actf(qtm[p_][:, :], acc[:, :], AF.Silu)
                            k.ts(QT[:, x0:x0 + 128], qtm[p_][:, :], float(128 ** -0.5), ALU.mult, E=pool)
                        else:
                            k.actf(KT_[:, x0:x0 + 128], acc[:, :], AF.Silu)
                    k.copy(V1[:, ci, 0:128], pv[:, 0:128], E=act)
                    pk = ps()
                    pk16 = psb16(pk)
                    k.tr(pk16[:, 0:128], KT_[:, x0:x0 + 128], C("ident", True))
                    return pk16

                pk_cur = m1X(0, 0)
                for ci in range(len(CHUNKS)):
                    pk_nxt = m1X(ci + 1, (ci + 1) % 2) if ci + 1 < len(CHUNKS) else None
                    k.copy(KTOK[:, ci, :], pk_cur[:, 0:128])
                    pk_cur = pk_nxt
                def stepA(d, s, ci, T):
                    Eb, Pb, kw, cols = T
                    dh = d * 4 + h
                    _, st = CHUNKS[ci]
                    hc = HOFF[s] + st
                    x0 = XOFF[s] + st
                    c0, cL = (0, 128) if d == 0 else (129, 1)
                    pa_ = ps()
                    k.mm(pa_[:, 0:128], KT_[:, x0:x0 + 128], QT[:, x0:x0 + 128])
                    pb_ = ps()
                    k.mm(pb_[:, 0:130], selt[:, dh * 128:(dh + 1) * 128], GR[:, hc - 1:hc + 129])
                    k.actf(Eb[:, :], pb_[:, 1:129], AF.Exp, bias=TK[:, ci, 0, dh:dh + 1], scale=-1.0)
                    k.copy(cols[:, 0:1], pb_[:, c0:c0 + 1])
                    k.ts(cols[:, 1:2], pb_[:, cL:cL + 1], -1.0, ALU.mult)
                    k.tt(Eb[:, :], Eb[:, :], C("up_incl" if d == 0 else "lo_incl"), ALU.mult)
                    k.actf(cols[:, 2:3], TK[:, ci, 1, dh:dh + 1], AF.Exp, bias=cols[:, 0:1], scale=-1.0)
                    k.actf(cols[:, 3:4], TK[:, ci, 2, dh:dh + 1], AF.Exp, scale=-1.0)
                    k.actf(cols[:, 6:7], TK[:, ci, 0, dh:dh + 1], AF.Exp, bias=cols[:, 1:2])
                    k.actf(cols[:, 7:8], cols[:, 0:1], AF.Exp, bias=cols[:, 1:2])
                    k.tt(Pb[:, :], pa_[:, 0:128], Eb[:, :], ALU.mult)
                    k.ts(kw[:, :], KTOK[:, ci, :], cols[:, 6:7], ALU.mult)

                def stepB(d, s, ci, T, cn, cnb):
                    Eb, Pb, kw, cols = T
                    _, st = CHUNKS[ci]
                    x0 = XOFF[s] + st
                    pd_ = ps()
                    k.mm(pd_[:, 0:129], QT[:, x0:x0 + 128], cnb[:, 0:129])
                    pe_ = ps()
                    k.mm(pe_[:, 0:129], kw[:, :], V1[:, ci, 0:129])
                    pc_ = ps()
                    k.mm(pc_[:, 0:129], Pb[:, :], V1[:, ci, 0:129])
                    k.actf(T1[:, 0:129], pd_[:, 0:129], AF.Identity, scale=cols[:, 2:3])
                    k.stt(cn[:, 0:129], cn[:, 0:129], cols[:, 7:8], pe_[:, 0:129], ALU.mult, ALU.add)
                    k.copy(cnb[:, 0:129], cn[:, 0:129], E=act)
                    k.tt(ND[:, 0:129], pc_[:, 0:129], T1[:, 0:129], ALU.add)
                    k.ts(cols[:, 4:5], ND[:, 128:129], -1.0, ALU.mult)
                    k.tt(cols[:, 4:5], cols[:, 4:5], ND[:, 128:129], ALU.max)
                    k.ts(cols[:, 4:5], cols[:, 4:5], cols[:, 3:4], ALU.max)
                    k.recip(cols[:, 5:6], cols[:, 4:5])
                    if d == 0:
                        k.ts(HS[:, ci, :], ND[:, 0:128], cols[:, 5:6], ALU.mult)
                    else:
                        k.stt(HS[:, ci, :], ND[:, 0:128], cols[:, 5:6], HS[:, ci, :], ALU.mult, ALU.add)

                for d in range(2):
                    for s in range(3):
                        cn, cnb = CN[s], CNb[s]
                        if s == 2:
                            k.dma(sp, cn[:, 0:128], d_stC[d, h, :, :])
                            k.dma(sp, cn[:, 128:129], V(d_stn.t[d, h, :].rearrange("(p o) -> p o", o=1), d_stn))
                        else:
                            k.memset(cn[:, :], 0.0)
                        k.copy(cnb[:, 0:129], cn[:, 0:129], E=act)
                        cis = [ci for ci, (s_, st_) in enumerate(CHUNKS) if s_ == s]
                        if d == 1:
                            cis = cis[::-1]
                        stepA(d, s, cis[0], TS[0])
                        for j_, ci in enumerate(cis):
                            if j_ + 1 < len(cis):
                                stepA(d, s, cis[j_ + 1], TS[(j_ + 1) % 2])
                            stepB(d, s, ci, TS[j_ % 2], cn, cnb)
                        if s < 2:
                            k.dma(sp, o_sC[s, d, h, :, :], cn[:, 0:128])
                            k.dma(sp, V(o_sn.t[s, d, h, :].rearrange("(p o) -> p o", o=1), o_sn), cn[:, 128:129])
                hnP = [hn, TS[0][0]]
                sgP = [sgo, TS[1][0]]
                mvP = [mv, T1]
                s6P = [st6, ND]

                def m3X(ci, par):
                    mvp = mvP[par]
                    bn(mvp[:, 0:2], HS[:, ci, :], s6P[par][:, 0:6])
                    k.actf(mvp[:, 1:2], mvp[:, 1:2], AF.Sqrt, bias=P("eps"))
                    k.recip(mvp[:, 1:2], mvp[:, 1:2])
                    k.ts(hnP[par][:, :], HS[:, ci, :], mvp[:, 0:1], ALU.subtract, mvp[:, 1:2], ALU.mult)

                def m3Y(ci, par):
                    s, st = CHUNKS[ci]
                    hc = HOFF[s] + st
                    x0 = XOFF[s] + st
                    po = ps()
                    for kc in range(8):
                        k.mm(po[:, 0:128], sl[:, kc, 384:512], Hany(kc, hc, 128), start=(kc == 0), stop=(kc == 7))
                    k.actf(sgP[par][:, :], po[:, 0:128], AF.Sigmoid)
                    pf_ = ps()
                    k.tr(pf_[:, 0:128], hnP[par][:, :], C("ident"))
                    k.stt(ymix[:, x0:x0 + 128], pf_[:, 0:128], P("ml_nw", h), sgP[par][:, :], ALU.mult, ALU.mult)

                m3X(0, 0)
                for ci in range(len(CHUNKS)):
                    if ci + 1 < len(CHUNKS):
                        m3X(ci + 1, (ci + 1) % 2)
                    m3Y(ci, ci % 2)
                out_proj(d_odout, 4 + h, ymix_tile, wout, l)

        def layer1_rwkv():
          l = 1
          with k.scope():
            ymix = k.sb("ymix", [128, NTOK], BF16)
            wout1 = k.sb("wout0", [128, D], BF16)
            wout = [wout1, wout1]

            def ymix_tile(ti):
                s, st, n = TILES[ti]
                return ymix[:, XOFF[s] + st:XOFF[s] + st + n]
            LW = k.sb("LW", [128, NTOK], BF16)
            LG = k.sb("LG", [128, NTOK], BF16)
            YA = k.sb("YA", [128, 20, 128])
            BON = k.sb("BON", [128, NTOK], BF16)
            lvm = k.sb("lvm", [128, 14 * 128], BF16)
            k.dma(pool, lvm[:, :], d_cst[:, 15 * 128:29 * 128])
            if os.environ.get("K_RWSMALL") or os.environ.get("K_RWDBG"):
                k.memset(YA[:, :, :].re("p a b -> p (a b)"), 0.0)
                k.memset(BON[:, :], 0.0)
            L2w = k.sb("L2w", [128, 2, 512], BF16)
            L2a = k.sb("L2a", [128, 2, 512], BF16)
            k.memset(L2w[:, :, :].re("p a b -> p (a b)"), 0.0)
            k.memset(L2a[:, :, :].re("p a b -> p (a b)"), 0.0)
            G2 = k.sb("G2", [128, 512], BF16)
            muc = k.sb("muc", [128, 14])
            nw0 = k.sb("nw0", [128, 2, 4])
            na0 = k.sb("na0", [128, 2, 4])
            cc = k.sb("cc", [128, 4])
            k.memset(cc[:, 0:1], -0.5)
            k.memset(cc[:, 1:2], 1.0)
            k.memset(cc[:, 2:3], 64e-5)
            for d in range(2):
                k.dma(pool, L2w[0:64, d, :], d_rww2[d * 64:(d + 1) * 64, :])
                k.dma(pool, L2a[64:128, d, :], d_rwa2[d * 64:(d + 1) * 64, :])
                k.ts(nw0[:, d, :], P(f"rw_w0{d}", 0, 4), -1.0, ALU.mult)
                k.ts(na0[:, d, :], P(f"rw_a0{d}", 0, 4), -1.0, ALU.mult)
            k.dma(pool, G2[:, :], d_rwg2[:, :])
            k.tt(muc[:, :], P("rw_mu0", 0, 14), P("rw_mu1", 0, 14), ALU.add)
            k.ts(muc[:, :], muc[:, :], -1.0, ALU.mult, 1.0, ALU.add)
            F = {n_: k.sb(f"rw_{n_}", [128, 128]) for n_ in
                 ["r", "k", "v", "sq", "kk", "e", "a", "cum", "E1", "E2", "E3", "E4", "t1", "kd", "ba"]}
            F["cumx"] = F["sq"]
            F["nrm"] = F["t1"]
            F["kkr"] = F["E4"]
            B = {n_: k.sb(f"rwb_{n_}", [128, 128], BF16) for n_ in ["AT", "BT", "KT", "RT", "BW", "KW", "vb"]}
            TKM2 = [k.sb(f"TKM{i_}", [128, 4, 128], BF16) for i_ in range(2)]
            WL = k.sb("rwWL", [128, 2])
            HB_ = []
            for hh in range(2):
                d_ = {n_: k.sb(f"rwh{hh}_{n_}", [128, 128], BF16) for n_ in
                      ["M0", "M1", "MT0", "MT1", "XT0", "XT1", "AAK"]}
                for n_ in ["ARB", "ARK", "XAK", "XAT", "RTz"]:
                    d_[n_] = [k.sb(f"rwh{hh}_{n_}{i_}", [128, 128], BF16) for i_ in range(2)]
                d_["XTf"] = k.sb(f"rwh{hh}_XTf", [128, 128])
                d_["Xf"] = k.sb(f"rwh{hh}_Xf", [128, 128])
                d_["ATz"] = k.sb(f"rwh{hh}_ATz", [128, 128], BF16)
                k.memset(d_["ATz"][:, :], 0.0)
                for i_ in range(2):
                    k.memset(d_["RTz"][i_][:, :], 0.0)
                    k.memset(d_["XAT"][i_][:, :], 0.0)
                sbz1 = k.sb(f"rwh{hh}_Sbz", [128, 64], BF16)
                k.memset(sbz1[:, :], 0.0)
                d_["Sbz"] = [sbz1, sbz1, sbz1]
                d_["Ub"] = k.sb(f"rwh{hh}_Ub", [128, 64], BF16)
                HB_.append(d_)
            st1 = k.sb("rwS", [128, 128])
            k.memset(st1[:, :], 0.0)
            ST = [st1, st1, st1]
            STb = [None] * 3
            stg = F["ba"]
            st6 = k.sb("rwst6", [128, 6])
            mv = k.sb("rwmv", [128, 4])
            yn = k.sb("rwyn", [128, 128])
            sto = yn

            def shift(dst, z, j):
                k.ts(dst, z[:, 1:129], muc[:, j:j + 1], ALU.mult)
                k.stt(dst, z[:, 0:128], P("rw_mu0", j), dst, ALU.mult, ALU.add)
                k.stt(dst, z[:, 2:130], P("rw_mu1", j), dst, ALU.mult, ALU.add)

            sl = wslot()
            load_w(sl, d_odin, 0, 8, 1536, 256, 0)
            for ci, (s, st) in enumerate(CHUNKS):
                hc = HOFF[s] + st
                x0 = XOFF[s] + st
                pz = ps()
                for q in range(2):
                    for kc in range(8):
                        k.mm(pz[:, q * 130:(q + 1) * 130], sl[:, kc, q * 128:(q + 1) * 128], Hany(kc, hc - 1, 130),
                             start=(kc == 0), stop=(kc == 7))
                shift(F["r"][:, :], pz[:, 0:130], 12)
                k.actf(LG[:, x0:x0 + 128], F["r"][:, :], AF.Sigmoid)
                shift(F["k"][:, :], pz[:, 130:260], 13)
                k.actf(LW[0:64, x0:x0 + 128], F["k"][0:64, :], AF.Tanh)
                k.copy(LW[64:128, x0:x0 + 128], F["k"][64:128, :], E=act)

            dbg = int(os.environ.get("K_RWDBG", "99"))

            def proj(ci, sl):
                s_, st_ = CHUNKS[ci]
                hc_ = HOFF[s_] + st_
                for q in range(3):
                    for kc in range(8):
                        k.mm(PS[0][:, q * 130:(q + 1) * 130], sl[:, kc, q * 128:(q + 1) * 128], Hany(kc, hc_ - 1, 130),
                             start=(kc == 0), stop=(kc == 7))

            def chunk_step(hp, d, s, ci, sl, nxt=None, par=0):
                TKM = TKM2[par]
                _, st = CHUNKS[ci]
                hc = HOFF[s] + st
                x0 = XOFF[s] + st
                S0, S0b = ST[s], STb[s]
                o = (lambda v: v) if d == 0 else rev
                last = 127 if d == 0 else 0
                r_, k_, v_, kkr, sq, nrm, kk, e_, a_, cum, cumx, E1, E2, E3, E4, t1, kd, ba = (
                    F[n_][:, :] for n_ in ["r", "k", "v", "kkr", "sq", "nrm", "kk", "e", "a", "cum", "cumx", "E1", "E2", "E3", "E4", "t1", "kd", "ba"])
                AT, BT, KTt, RT, BW, KW, vb = (B[n_][:, :] for n_ in ["AT", "BT", "KT", "RT", "BW", "KW", "vb"])
                pw = ps()
                k.mm(pw[:, 0:128], L2w[:, d, hp * 128:(hp + 1) * 128], LW[:, x0:x0 + 128])
                k.mm(pw[:, 128:256], L2a[:, d, hp * 128:(hp + 1) * 128], LW[:, x0:x0 + 128])
                pz = PS[0]
                k.softplus(e_, pw[:, 0:128], bias=nw0[:, d, hp:hp + 1], scale=-1.0, composite=True)
                k.actf(e_, e_, AF.Exp, bias=cc[:, 0:1], scale=-1.0)
                k.actf(a_, pw[:, 128:256], AF.Exp, bias=na0[:, d, hp:hp + 1], scale=-1.0)
                shift(r_, pz[:, 0:130], hp)
                shift(k_, pz[:, 130:260], 4 + hp)
                k.ts(kkr, k_, P("rw_kk", hp), ALU.mult)
                k.tt(sq, kkr, kkr, ALU.mult)
                pn = ps()
                k.mm(pn[:, 0:128], C("blk64"), sq)
                k.actf(nrm, pn[:, 0:128], AF.Ln)
                k.actf(nrm, nrm, AF.Exp, scale=-0.5)
                shift(v_, pz[:, 260:390], 8 + hp)
                if nxt is not None:
                    proj(nxt, sl)
                k.ts(a_, a_, 1.0, ALU.add)
                k.recip(a_, a_)
                ones_b = V(cc.t[:, 1:2].broadcast_to([128, 128]), cc)
                k.ts(e_, e_, -1.0, ALU.mult)
                k.scan(o(cum), ones_b, o(e_), 0.0, ALU.mult, ALU.add)
                k.ts(nrm, nrm, 1e12, ALU.min)
                k.tt(kk, kkr, nrm, ALU.mult)
                k.tt(cumx, cum, e_, ALU.subtract)
                k.actf(E3, cum, AF.Exp)
                k.copy(WL[:, par:par + 1], E3[:, last:last + 1], E=act)
                k.actf(E2, cum, AF.Exp, scale=-1.0)
                k.actf(E4, cum, AF.Exp, scale=-1.0, bias=cum[:, last:last + 1])
                k.actf(E1, cumx, AF.Exp)
                k.ts(t1, a_, -1.0, ALU.add, P("rw_ka", hp), ALU.mult)
                k.ts(t1, t1, 1.0, ALU.add)
                k.tt(kd, k_, t1, ALU.mult)
                k.tt(ba, kk, a_, ALU.mult)
                k.tt(RT, r_, E3, ALU.mult)
                k.tt(BT, ba, E2, ALU.mult)
                k.tt(KTt, kd, E2, ALU.mult)
                k.tt(BW, ba, E4, ALU.mult)
                k.tt(KW, kd, E4, ALU.mult)
                k.tt(AT, kk, E1, ALU.mult)
                k.copy(vb, v_, E=dve)
                k.stt(t1, r_, P("rw_rk", hp), kd, ALU.mult, ALU.mult)
                pbn = ps()
                k.mm(pbn[:, 0:128], C("blk64"), t1)
                if d == 0:
                    k.tt(BON[:, x0:x0 + 128], pbn[:, 0:128], v_, ALU.mult)
                else:
                    k.tt(sq, pbn[:, 0:128], v_, ALU.mult)
                    k.tt(BON[:, x0:x0 + 128], BON[:, x0:x0 + 128], sq, ALU.add)
                if dbg < 5:
                    return
                ptb = ps()
                p16 = psb16(ptb)
                for q, src in enumerate([AT, BW, KW, vb]):
                    k.tr(p16[:, q * 128:(q + 1) * 128], src, C("ident", True))
                k.copy(TKM[:, :, :].re("p a b -> p (a b)"), p16[:, 0:512], E=act)
                if dbg < 6:
                    return
                msk_s = C("lo_strict" if d == 0 else "up_strict")
                mskT_s = C("up_strict" if d == 0 else "lo_strict")
                mskT_i = C("up_incl" if d == 0 else "lo_incl")
                cur = [0, 0]

                def LMa(lv):
                    j = lv if d == 0 else 7 + lv
                    return lvm[:, j * 128:(j + 1) * 128]

                def LMb(lv):
                    j = 7 + lv if d == 0 else lv
                    return lvm[:, j * 128:(j + 1) * 128]
                p1s, p2s = [], []
                for hh in range(2):
                    PR = slice(hh * 64, hh * 64 + 64)
                    H_ = HB_[hh]
                    k.copy(H_["RTz"][par][PR, :], RT[PR, :], E=act)
                    k.copy(H_["ATz"][PR, :], AT[PR, :], E=act)
                for hh in range(2):
                    H_ = HB_[hh]
                    ATz, RTz = H_["ATz"][:, :], H_["RTz"][par][:, :]
                    p1 = ps()
                    p1s.append(p1)
                    k.mm(p1[:, 0:128], ATz, BT)
                    k.mm(p1[:, 128:256], BT, ATz)
                    k.mm(p1[:, 256:384], ATz, KTt)
                    p2 = ps()
                    p2s.append(p2)
                    k.mm(p2[:, 0:128], BT, RTz)
                    k.mm(p2[:, 128:256], KTt, RTz)
                for hh in range(2):
                    H_ = HB_[hh]
                    p1 = p1s[hh]
                    k.stt(H_["M0"][:, :], p1[:, 0:128], -1.0, msk_s, ALU.mult, ALU.mult)
                    k.stt(H_["MT0"][:, :], p1[:, 128:256], -1.0, mskT_s, ALU.mult, ALU.mult)
                    k.stt(H_["XT1"][:, :], p1[:, 0:128], -1.0, LMa(0), ALU.mult, ALU.mult)
                    k.tt(H_["XT1"][:, :], H_["XT1"][:, :], C("ident", True), ALU.add)
                    k.stt(H_["XT0"][:, :], p1[:, 128:256], -1.0, LMb(0), ALU.mult, ALU.mult)
                    k.tt(H_["XT0"][:, :], H_["XT0"][:, :], C("ident", True), ALU.add)
                for hh in range(2):
                    H_ = HB_[hh]
                    k.stt(H_["AAK"][:, :], p1s[hh][:, 256:384], -1.0, msk_s, ALU.mult, ALU.mult)
                    k.tt(H_["ARB"][par][:, :], p2s[hh][:, 0:128], mskT_i, ALU.mult)
                    k.tt(H_["ARK"][par][:, :], p2s[hh][:, 128:256], mskT_i, ALU.mult)
                if dbg < 7:
                    return (lambda: None)
                for lv in range(1, 7):
                    pTs, pXs = [], []
                    for hh in range(2):
                        H_ = HB_[hh]
                        pT = ps()
                        pTs.append(pT)
                        if lv < 6:
                            k.mm(pT[:, 0:128], H_["MT0"][:, :], H_["XT1"][:, :])
                        k.mm(pT[:, 128:256], H_["M0"][:, :], H_["XT0"][:, :])
                    for hh in range(2):
                        H_ = HB_[hh]
                        pT = pTs[hh]
                        if lv < 6:
                            k.tt(H_["M1"][:, :], pT[:, 0:128], LMa(lv), ALU.mult)
                        k.tt(H_["MT1"][:, :], pT[:, 128:256], LMb(lv), ALU.mult)
                    for hh in range(2):
                        H_ = HB_[hh]
                        pX = ps()
                        pXs.append(pX)
                        if lv < 6:
                            k.mm(pX[:, 0:128], C("ident", True), H_["XT1"][:, :], start=True, stop=False)
                            k.mm(pX[:, 0:128], H_["XT0"][:, :], H_["M1"][:, :], start=False, stop=True)
                        k.mm(pX[:, 128:256], C("ident", True), H_["XT0"][:, :], start=True, stop=False)
                        k.mm(pX[:, 128:256], H_["XT1"][:, :], H_["MT1"][:, :], start=False, stop=True)
                    for hh in range(2):
                        H_ = HB_[hh]
                        pX = pXs[hh]
                        k.copy(H_["XT0"][:, :], pX[:, 128:256], E=act)
                        if lv < 6:
                            k.copy(H_["XT1"][:, :], pX[:, 0:128], E=act)
                        cur[hh] = 0
                if dbg < 8:
                    return (lambda: None)
                PRs = [slice(0, 64), slice(64, 128)]
                Vks = [TKM[:, 3, 0:64], TKM[:, 3, 64:128]]
                Sbzs = [HB_[hh]["Sbz"][s][:, :] for hh in range(2)]
                pQs, pQ2s = [], []
                for hh in range(2):
                    H_ = HB_[hh]
                    XT = H_["XT0"][:, :]
                    pQ = ps()
                    pQs.append(pQ)
                    k.mm(pQ[:, 0:128], TKM[:, 0, :], XT)
                    pQ2 = ps()
                    pQ2s.append(pQ2)
                    k.mm(pQ2[:, 0:128], H_["AAK"][:, :], XT)
                for hh in range(2):
                    H_ = HB_[hh]
                    k.ts(H_["XAT"][par][PRs[hh], :], pQs[hh][PRs[hh], 0:128], -1.0, ALU.mult)
                    k.copy(H_["XAK"][par][:, :], pQ2s[hh][:, 0:128], E=act)

                def tail():
                    pUs, pYs, pSs = [], [], []
                    for hh in range(2):
                        H_ = HB_[hh]
                        pU = ps()
                        pUs.append(pU)
                        k.mm(pU[:, 0:64], H_["XAT"][par][:, :], Sbzs[hh], start=True, stop=False)
                        k.mm(pU[:, 0:64], H_["XAK"][par][:, :], Vks[hh], start=False, stop=True)
                    for hh in range(2):
                        k.copy(HB_[hh]["Ub"][:, :], pUs[hh][:, 0:64], E=(act if hh == 0 else dve))
                    for hh in range(2):
                        H_ = HB_[hh]
                        pS = ps()
                        pSs.append(pS)
                        k.mm(pS[:, 0:64], TKM[:, 1, :], H_["Ub"][:, :], start=True, stop=False)
                        k.mm(pS[:, 0:64], TKM[:, 2, :], Vks[hh], start=False, stop=True)
                        pY = ps()
                        pYs.append(pY)
                        k.mm(pY[:, 0:64], H_["RTz"][par][:, :], Sbzs[hh], start=True, stop=False)
                        k.mm(pY[:, 0:64], H_["ARB"][par][:, :], H_["Ub"][:, :], start=False, stop=False)
                        k.mm(pY[:, 0:64], H_["ARK"][par][:, :], Vks[hh], start=False, stop=True)
                    for hh in range(2):
                        PR = PRs[hh]
                        k.stt(S0[PR, 0:64], S0[PR, 0:64], WL[PR, par:par + 1], pSs[hh][PR, 0:64], ALU.mult, ALU.add)
                        k.copy(Sbzs[hh][PR, :], S0[PR, 0:64], E=act)
                    for hh in range(2):
                        pb = hh * 64
                        if d == 0:
                            k.copy(YA[:, ci, pb:pb + 64], pYs[hh][:, 0:64], E=act)
                        else:
                            k.tt(YA[:, ci, pb:pb + 64], YA[:, ci, pb:pb + 64], pYs[hh][:, 0:64], ALU.add)
                return tail

            small = bool(os.environ.get("K_RWSMALL"))
            cfg = os.environ.get("K_RWCFG")
            if cfg:
                c_hps, c_ds, c_ss = [[int(x) for x in part.split(",")] for part in cfg.split(";")]
            else:
                c_hps = list(range(1 if (dbg < 50 or small) else 4))
                c_ds = list(range(2 if dbg >= 50 else 1))
                c_ss = list(range(1 if (dbg < 50 or small) else 3))
            def rw_load(hp_):
                sl_ = wslot()
                for q in range(3):
                    load_w(sl_, d_odin, 0, 8, q * 512 + hp_ * 128, 128, q * 128)
                return sl_
            sl_next = rw_load(c_hps[0])
            for hpi_, hp in enumerate(c_hps):
                if dbg < 1:
                    break
                sl = sl_next
                sl_next = rw_load(c_hps[hpi_ + 1]) if hpi_ + 1 < len(c_hps) else None
                steps = []
                for d in c_ds:
                    for s in c_ss:
                        cis = [ci for ci, (s_, st_) in enumerate(CHUNKS) if s_ == s]
                        if d == 1:
                            cis = cis[::-1]
                        for j_, ci in enumerate(cis):
                            steps.append((d, s, ci, j_ == 0, j_ == len(cis) - 1))
                ps_reserve[0] = True
                proj(steps[0][2], sl)

                def runA(si_):
                    d_, s_, ci_, _, _ = steps[si_]
                    nx_ = steps[si_ + 1][2] if si_ + 1 < len(steps) else None
                    return chunk_step(hp, d_, s_, ci_, sl, nx_, si_ % 2)
                tail_cur = runA(0)
                for si_, (d, s, ci, first_, last_) in enumerate(steps):
                    tail_nxt = runA(si_ + 1) if si_ + 1 < len(steps) else None
                    S0 = ST[s]
                    if first_:
                        if s == 2 and not os.environ.get("K_RWNOSTATE"):
                            k.memset(stg[:, :], 0.0)
                            for hh in range(2):
                                k.dma(sp, stg[0:64, hh * 64:(hh + 1) * 64], d_strw[d, 2 * hp + hh, :, :])
                            pt = ps()
                            k.tr(pt[:, 0:128], stg[:, :], C("ident"))
                            k.copy(S0[:, 0:64], pt[:, 0:64])
                        else:
                            k.memset(S0[:, 0:64], 0.0)
                        for hh in range(2):
                            k.copy(HB_[hh]["Sbz"][s][hh * 64:(hh + 1) * 64, :], S0[hh * 64:(hh + 1) * 64, 0:64], E=act)
                    tail_cur()
                    tail_cur = tail_nxt
                    if last_ and s < 2:
                        pt = ps()
                        k.tr(pt[:, 0:128], S0[:, :], C("ident"))
                        k.copy(sto[0:64, :], pt[0:64, 0:128])
                        for hh in range(2):
                            k.dma(sp, o_srw[s, d, 2 * hp + hh, :, :], sto[0:64, hh * 64:(hh + 1) * 64])
                ps_reserve[0] = False
                ynP = [yn, F["r"]]
                t1P = [F["t1"], F["k"]]
                mvP = [mv, F["v"]]
                s6P = [st6, F["kk"]]

                def finX(ci, par):
                    for hh in range(2):
                        mvp = mvP[par]
                        bn(mvp[:, hh * 2:hh * 2 + 2], YA[:, ci, hh * 64:(hh + 1) * 64], s6P[par][:, 0:6])
                        k.actf(mvp[:, hh * 2 + 1:hh * 2 + 2], mvp[:, hh * 2 + 1:hh * 2 + 2], AF.Ln, bias=cc[:, 2:3])
                        k.actf(mvp[:, hh * 2 + 1:hh * 2 + 2], mvp[:, hh * 2 + 1:hh * 2 + 2], AF.Exp, scale=-0.5)
                        k.ts(ynP[par][:, hh * 64:(hh + 1) * 64], YA[:, ci, hh * 64:(hh + 1) * 64], mvp[:, hh * 2:hh * 2 + 1],
                             ALU.subtract, mvp[:, hh * 2 + 1:hh * 2 + 2], ALU.mult)

                def finY(ci, par):
                    s, st = CHUNKS[ci]
                    x0 = XOFF[s] + st
                    pT = ps()
                    k.tr(pT[:, 0:128], ynP[par][:, :], C("ident"))
                    pg = ps()
                    k.mm(pg[:, 0:128], G2[:, hp * 128:(hp + 1) * 128], LG[:, x0:x0 + 128])
                    t1 = t1P[par][:, :]
                    k.ts(t1, pT[:, 0:128], P("rw_lnw", hp), ALU.mult, P("rw_lnb", hp), ALU.add)
                    k.tt(t1, t1, BON[:, x0:x0 + 128], ALU.add)
                    k.tt(ymix[:, x0:x0 + 128], t1, pg[:, 0:128], ALU.mult)

                finX(0, 0)
                for ci in range(len(CHUNKS)):
                    if ci + 1 < len(CHUNKS):
                        finX(ci + 1, (ci + 1) % 2)
                    finY(ci, ci % 2)
                out_proj(d_odout, hp, ymix_tile, wout, l)

        norm_mod(0, 0)
        if stage >= 1 and stage != 66:
            layer0_mixers()
        if stage != 66:
            norm_mod(0, 1)
            mlp(0)
        if stage >= 5:
            if stage != 66:
                norm_mod(1, 0)
            if stage not in (6, 66):
                layer1_mlstm()
            if stage in (6, 7, 9, 66):
                layer1_rwkv()
            if stage != 66:
                norm_mod(1, 1)
                mlp(1)

        final_norm = stage >= 9
        k.scopes.append(ExitStack())
        norm_alloc()
        ostg = [k.sb(f"ostg{i}", [128, D]) for i in range(2)]
        ftmp = [k.sb(f"ftmp{i}", [128, 128]) for i in range(8)]
        fi = 0
        for ti in range(NT):
            s, st, n = TILES[ti]
            rs = norm_tile(ti) if final_norm else None
            for b4 in range(n // 128):
                og = ostg[(fi) % 2]
                fi += 1
                for half in range(2):
                    pt = ps()
                    for q in range(4):
                        c = half * 4 + q
                        src = V(xT.t[:, c, XOFF[s] + st + b4 * 128:XOFF[s] + st + (b4 + 1) * 128], XB[c][ti])
                        if final_norm:
                            ft = ftmp[c]
                            k.stt(ft[:, :], src, P("fnw", c), rs[:, b4 * 128:(b4 + 1) * 128], ALU.mult, ALU.mult)
                            src = ft[:, :]
                        k.tr(pt[:, q * 128:(q + 1) * 128], src, C("ident"))
                    k.copy(og[:, half * 512:(half + 1) * 512], pt[:, :], E=(act if half == 0 else dve))
                row0 = XOFF[s] + st + b4 * 128
                if row0 < 512:
                    k.dma(sp, o_yp[row0:row0 + 128, :], og[:, :])
                else:
                    k.dma(sp, o_ys[row0 - 512:row0 - 512 + 128, :], og[:, :])
        k.finish(outs)
        print("nsem", k.nsem, "instr", {e.name: e.cnt for e in [k.pe, k.act, k.dve, k.pool]})
        while k.scopes:
            k.scopes.pop().close()
    return nc


_CACHE = {}


def kernel(**inp):
    stage = int(os.environ.get("K_STAGE", "9"))
    inp = {k_: np.asarray(v) for k_, v in inp.items()}
    packs = []
    for core in range(NCORES):
        P = pack_params(inp, core)
        P.add_rows("eps", np.full((128, 1), EPS, np.float32))
        packs.append(P)
    pidx = packs[0].idx
    npar = packs[0].n
    key = (npar, stage)
    if key not in _CACHE:
        _CACHE[key] = build(pidx, npar, stage)
    nc = _CACHE[key]
    shared = {
        "cst": CONST_ARR,
        "w_mod": np.ascontiguousarray(inp["w_mod"].reshape(2 * D, 6 * D)),
        "mlp_w1": np.ascontiguousarray(inp["mlp_w1"].reshape(2 * D, 4 * D)),
        "mlp_w2": np.ascontiguousarray(inp["mlp_w2"].reshape(2 * 4 * D, D)),
        "ev_w_in": np.ascontiguousarray(inp["ev_w_in"][0]),
        "ev_w_out": np.ascontiguousarray(inp["ev_w_out"][0]),
        "pool_w": np.ascontiguousarray(inp["pool_w"][0].reshape(512, 128)),
        "rg_wa": np.ascontiguousarray(inp["rg_wa"][0].reshape(1024, 64)),
        "rg_wx": np.ascontiguousarray(inp["rg_wx"][0].reshape(1024, 64)),
        "od_w_in": np.ascontiguousarray(inp["od_w_in"][0]),
        "od_w_out": np.ascontiguousarray(inp["od_w_out"][0]),
        "rw_w2": np.ascontiguousarray(inp["rw_w2"][0].reshape(128, 512)),
        "rw_a2": np.ascontiguousarray(inp["rw_a2"][0].reshape(128, 512)),
        "rw_g2": np.ascontiguousarray(inp["rw_g2"][0]),
    }
    in_maps = []
    for core in range(NCORES):
        m = dict(shared)
        m["xp"] = np.ascontiguousarray(inp["x_prompt"][2 * core:2 * core + 2].reshape(512, D))
        m["xs"] = np.ascontiguousarray(inp["x_sample"][core].reshape(2048, D))
        m["par"] = packs[core].build()
        m["st_rw"] = np.ascontiguousarray(inp["state_rwkv"][core, 0])
        m["st_C"] = np.ascontiguousarray(inp["state_mlstm_C"][core, 0])
        m["st_n"] = np.ascontiguousarray(inp["state_mlstm_n"][core, 0])
        in_maps.append(m)
    res = run_bass_kernel_spmd(nc, in_maps, core_ids=list(range(NCORES)))
    R = res.results
    y_prompt = np.concatenate([r["yp"].reshape(2, 256, D) for r in R], axis=0)
    y_sample = np.stack([r["ys"] for r in R], axis=0)
    s_rg = np.concatenate([r["s_rg"].reshape(2, 1, 2, 512) for r in R], axis=0)
    s_rw = np.concatenate([r["s_rw"].reshape(2, 1, 2, 8, 64, 64) for r in R], axis=0)
    s_C = np.concatenate([r["s_C"].reshape(2, 1, 2, 4, 128, 128) for r in R], axis=0)
    s_n = np.concatenate([r["s_n"].reshape(2, 1, 2, 4, 128) for r in R], axis=0)
    s_m = np.concatenate([r["s_m"].reshape(2, 1, 2, 4) for r in R], axis=0)
    return (y_prompt.astype(np.float32), y_sample.astype(np.float32), s_rg.astype(np.float32),
            s_rw.astype(np.float32), s_C.astype(np.float32), s_n.astype(np.float32), s_m.astype(np.float32))
```
